# Optimizing a Trainium2 kernel written in Bass

```python
import math
import jax, jax.numpy as jnp
from jax import lax
import numpy as np

D_MODEL = 1024
BATCH = 16
SEQ = 4096
DEPTH = 1

RWKV_HEADS = 8
RWKV_HEAD_DIM = 64
RWKV_DIM = RWKV_HEADS * RWKV_HEAD_DIM
DECAY_LORA = 64
AAA_LORA = 64
GATE_LORA = 128
GN_EPS = 64e-5
MLA_HEADS = 8
QK_NOPE_DIM = 64
QK_ROPE_DIM = 32
QK_HEAD_DIM = QK_NOPE_DIM + QK_ROPE_DIM
V_HEAD_DIM = 64
MLA_DIM = MLA_HEADS * V_HEAD_DIM
Q_LORA = 512
KV_LORA = 256
ROPE_THETA = 10000.0
Q_BLOCK = 128
D_FF = int(math.ceil(8 * D_MODEL / 3 / 256)) * 256
LN_EPS = 1e-5
RMS_EPS = 1e-6
ALPHA = (2.0 * DEPTH) ** 0.25
BETA = (8.0 * DEPTH) ** -0.25
RWKV_SPLITS = (RWKV_DIM, RWKV_DIM, RWKV_DIM, DECAY_LORA, AAA_LORA, GATE_LORA)
SHIFT_COLS = 3 * RWKV_DIM + DECAY_LORA + AAA_LORA + GATE_LORA
REST_SPLITS = (Q_LORA, KV_LORA, QK_ROPE_DIM, D_MODEL, D_MODEL)
D_IN = SHIFT_COLS + Q_LORA + KV_LORA + QK_ROPE_DIM + 2 * D_MODEL

kernel_name = "hybrid_rwkv7_mla_gated_deepnorm"


def _split(z, sizes):
    outs, start = [], 0
    for s in sizes:
        outs.append(z[..., start:start + s])
        start += s
    return outs


def layer_norm(x, g, b, eps=LN_EPS):
    xf = x.astype(jnp.float32)
    mu = jnp.mean(xf, axis=-1, keepdims=True)
    var = jnp.mean(jnp.square(xf - mu), axis=-1, keepdims=True)
    return ((xf - mu) * lax.rsqrt(var + eps) * g.astype(jnp.float32) + b.astype(jnp.float32)).astype(x.dtype)


def rms_norm(x, g, eps=RMS_EPS):
    xf = x.astype(jnp.float32)
    ms = jnp.mean(jnp.square(xf), axis=-1, keepdims=True)
    return (xf * lax.rsqrt(ms + eps) * g.astype(jnp.float32)).astype(x.dtype)


def rope_cos_sin(positions):
    inv_freq = ROPE_THETA ** (-jnp.arange(0, QK_ROPE_DIM, 2, dtype=jnp.float32) / QK_ROPE_DIM)
    ang = positions.astype(jnp.float32)[..., None] * inv_freq
    return jnp.cos(ang), jnp.sin(ang)


def apply_rope(x, cos, sin):
    half = x.shape[-1] // 2
    x1, x2 = x[..., :half], x[..., half:]
    cos = cos.astype(x.dtype)
    sin = sin.astype(x.dtype)
    return jnp.concatenate([x1 * cos - x2 * sin, x2 * cos + x1 * sin], axis=-1)


def rwkv7_scan(r, w, k, v, a, b):
    B, S, H, N = r.shape
    to_time = lambda t: jnp.moveaxis(t.astype(jnp.float32), 1, 0)
    xs = tuple(to_time(t) for t in (r, w, k, v, a, b))

    def step(state, inp):
        r_t, w_t, k_t, v_t, a_t, b_t = inp
        sa = jnp.einsum('bhvk,bhk->bhv', state, a_t)
        state = state * w_t[:, :, None, :] + sa[..., None] * b_t[:, :, None, :] + v_t[..., None] * k_t[:, :, None, :]
        y = jnp.einsum('bhvk,bhk->bhv', state, r_t)
        return state, y

    state0 = jnp.zeros((B, H, N, N), jnp.float32)
    _, ys = lax.scan(step, state0, xs)
    return jnp.moveaxis(ys, 0, 1).astype(r.dtype)


def rwkv7_branch(z_r, z_k, z_v, z_wd, z_ad, z_gd, w_decay_up, w_decay_base, w_aaa_up, w_aaa_base,
                 w_gate_up, k_k, k_a, r_k, lnx_g, lnx_b):
    B, S, _ = z_r.shape
    H, N = RWKV_HEADS, RWKV_HEAD_DIM
    w_log = -jax.nn.softplus(-(w_decay_base + jnp.tanh(z_wd) @ w_decay_up)) - 0.5
    decay = jnp.exp(-jnp.exp(w_log.astype(jnp.float32)))
    a = jax.nn.sigmoid(w_aaa_base + z_ad @ w_aaa_up)
    g = jax.nn.sigmoid(z_gd) @ w_gate_up
    kk = (z_k * k_k).reshape(B, S, H, N)
    kk_f = kk.astype(jnp.float32)
    kk = (kk_f / jnp.maximum(jnp.sqrt(jnp.sum(kk_f * kk_f, axis=-1, keepdims=True)), 1e-12)).astype(z_k.dtype)
    k = z_k * (1.0 + (a - 1.0) * k_a)
    heads = lambda t: t.reshape(B, S, H, N)
    r, k, v, a_h, decay = heads(z_r), heads(k), heads(z_v), heads(a), heads(decay)
    y = rwkv7_scan(r, decay, k, v, -kk, kk * a_h)
    y = layer_norm(y, lnx_g.reshape(H, N), lnx_b.reshape(H, N), eps=GN_EPS)
    bonus = jnp.sum(r * k * r_k, axis=-1, keepdims=True) * v
    return (y + bonus).reshape(B, S, RWKV_DIM) * g


def mla_branch(c_q, c_kv, k_pe_raw, positions, q_norm_g, w_uq, kv_norm_g, w_ukv):
    B, S, _ = c_q.shape
    H = MLA_HEADS
    cos, sin = rope_cos_sin(positions)
    q = (rms_norm(c_q, q_norm_g) @ w_uq).reshape(B, S, H, QK_HEAD_DIM)
    q_nope, q_pe = q[..., :QK_NOPE_DIM], q[..., QK_NOPE_DIM:]
    q_pe = apply_rope(q_pe, cos[:, :, None, :], sin[:, :, None, :])
    kv = (rms_norm(c_kv, kv_norm_g) @ w_ukv).reshape(B, S, H, QK_NOPE_DIM + V_HEAD_DIM)
    k_nope, v = kv[..., :QK_NOPE_DIM], kv[..., QK_NOPE_DIM:]
    k_pe = apply_rope(k_pe_raw, cos, sin)
    scale = QK_HEAD_DIM ** -0.5
    nb = S // Q_BLOCK
    to_blocks = lambda t: jnp.moveaxis(t.reshape(B, nb, Q_BLOCK, H, t.shape[-1]), 1, 0)
    key_idx = jnp.arange(S)

    def attend(args):
        qn, qp, blk = args
        s = jnp.einsum('bqhd,bkhd->bhqk', qn, k_nope) + jnp.einsum('bqhr,bkr->bhqk', qp, k_pe)
        s = s.astype(jnp.float32) * scale
        q_idx = blk * Q_BLOCK + jnp.arange(Q_BLOCK)
        mask = key_idx[None, :] <= q_idx[:, None]
        p = jax.nn.softmax(jnp.where(mask, s, -jnp.inf), axis=-1).astype(v.dtype)
        return jnp.einsum('bhqk,bkhd->bqhd', p, v)

    o = lax.map(attend, (to_blocks(q_nope), to_blocks(q_pe), jnp.arange(nb)))
    return jnp.moveaxis(o, 0, 1).reshape(B, S, MLA_DIM)


def hybrid_layer(x, positions, w_in, mu_shift, w_decay_up, w_decay_base, w_aaa_up, w_aaa_base, w_gate_up,
                 k_k, k_a, r_k, lnx_g, lnx_b, q_norm_g, w_uq, kv_norm_g, w_ukv, w_proj_rwkv, w_proj_mla,
                 w_out, ln1_g, ln1_b, w_ffn_gate, w_ffn_up, w_ffn_down, ln2_g, ln2_b):
    z = x @ w_in
    zs, zm = z[..., :SHIFT_COLS], z[..., SHIFT_COLS:]
    zs_prev = jnp.pad(zs, ((0, 0), (1, 0), (0, 0)))[:, :-1]
    zs = zs + mu_shift * (zs_prev - zs)
    z_r, z_k, z_v, z_wd, z_ad, z_gd = _split(zs, RWKV_SPLITS)
    c_q, c_kv, k_pe_raw, gate_r, gate_m = _split(zm, REST_SPLITS)
    y_r = rwkv7_branch(z_r, z_k, z_v, z_wd, z_ad, z_gd, w_decay_up, w_decay_base, w_aaa_up, w_aaa_base,
                       w_gate_up, k_k, k_a, r_k, lnx_g, lnx_b)
    y_m = mla_branch(c_q, c_kv, k_pe_raw, positions, q_norm_g, w_uq, kv_norm_g, w_ukv)
    merged = jax.nn.sigmoid(gate_r) * (y_r @ w_proj_rwkv) + jax.nn.sigmoid(gate_m) * (y_m @ w_proj_mla)
    h = layer_norm(ALPHA * x + merged @ w_out, ln1_g, ln1_b)
    ffn = (jax.nn.silu(h @ w_ffn_gate) * (h @ w_ffn_up)) @ w_ffn_down
    return layer_norm(ALPHA * h + ffn, ln2_g, ln2_b)


def setup_inputs(seed: int = 0) -> dict:
    key = jax.random.key(seed)
    ks = jax.random.split(key, 32)
    L = DEPTH
    f32 = jnp.float32

    def nrm(k, shape, scale):
        return jax.random.normal(k, shape, f32) * scale

    x = jax.random.normal(ks[0], (BATCH, SEQ, D_MODEL), f32)
    offset = jax.random.randint(ks[1], (BATCH, 1), 0, 1024, dtype=jnp.int32)
    positions = offset + jnp.arange(SEQ, dtype=jnp.int32)[None, :]
    col_scale = jnp.ones((D_IN,), f32).at[2 * RWKV_DIM:3 * RWKV_DIM].set(BETA)
    w_in = nrm(ks[2], (L, D_MODEL, D_IN), D_MODEL ** -0.5) * col_scale
    mu_shift = jax.random.uniform(ks[3], (L, SHIFT_COLS), f32)
    w_decay_up = nrm(ks[4], (L, DECAY_LORA, RWKV_DIM), 0.1 * DECAY_LORA ** -0.5)
    w_decay_base = jax.random.uniform(ks[5], (L, RWKV_DIM), f32, minval=-6.0, maxval=-1.0)
    w_aaa_up = nrm(ks[6], (L, AAA_LORA, RWKV_DIM), AAA_LORA ** -0.5)
    w_aaa_base = nrm(ks[7], (L, RWKV_DIM), 0.1)
    w_gate_up = nrm(ks[8], (L, GATE_LORA, RWKV_DIM), GATE_LORA ** -0.5)
    k_k = 0.85 + nrm(ks[9], (L, RWKV_DIM), 0.02)
    k_a = 1.0 + nrm(ks[10], (L, RWKV_DIM), 0.02)
    r_k = nrm(ks[11], (L, RWKV_HEADS, RWKV_HEAD_DIM), 0.1)
    lnx_g = 1.0 + nrm(ks[12], (L, RWKV_DIM), 0.02)
    lnx_b = nrm(ks[13], (L, RWKV_DIM), 0.02)
    q_norm_g = 1.0 + nrm(ks[14], (L, Q_LORA), 0.02)
    w_uq = nrm(ks[15], (L, Q_LORA, MLA_HEADS * QK_HEAD_DIM), Q_LORA ** -0.5)
    kv_norm_g = 1.0 + nrm(ks[16], (L, KV_LORA), 0.02)
    kv_col_scale = jnp.tile(jnp.concatenate([jnp.ones((QK_NOPE_DIM,), f32), jnp.full((V_HEAD_DIM,), BETA, f32)]), MLA_HEADS)
    w_ukv = nrm(ks[17], (L, KV_LORA, MLA_HEADS * (QK_NOPE_DIM + V_HEAD_DIM)), KV_LORA ** -0.5) * kv_col_scale
    w_proj_rwkv = nrm(ks[18], (L, RWKV_DIM, D_MODEL), BETA * RWKV_DIM ** -0.5)
    w_proj_mla = nrm(ks[19], (L, MLA_DIM, D_MODEL), BETA * MLA_DIM ** -0.5)
    w_out = nrm(ks[20], (L, D_MODEL, D_MODEL), BETA * D_MODEL ** -0.5)
    ln1_g = 1.0 + nrm(ks[21], (L, D_MODEL), 0.02)
    ln1_b = nrm(ks[22], (L, D_MODEL), 0.02)
    w_ffn_gate = nrm(ks[23], (L, D_MODEL, D_FF), BETA * D_MODEL ** -0.5)
    w_ffn_up = nrm(ks[24], (L, D_MODEL, D_FF), BETA * D_MODEL ** -0.5)
    w_ffn_down = nrm(ks[25], (L, D_FF, D_MODEL), BETA * D_FF ** -0.5)
    ln2_g = 1.0 + nrm(ks[26], (L, D_MODEL), 0.02)
    ln2_b = nrm(ks[27], (L, D_MODEL), 0.02)
    return {"x": x, "positions": positions, "w_in": w_in, "mu_shift": mu_shift,
            "w_decay_up": w_decay_up, "w_decay_base": w_decay_base, "w_aaa_up": w_aaa_up,
            "w_aaa_base": w_aaa_base, "w_gate_up": w_gate_up, "k_k": k_k, "k_a": k_a, "r_k": r_k,
            "lnx_g": lnx_g, "lnx_b": lnx_b, "q_norm_g": q_norm_g, "w_uq": w_uq,
            "kv_norm_g": kv_norm_g, "w_ukv": w_ukv, "w_proj_rwkv": w_proj_rwkv,
            "w_proj_mla": w_proj_mla, "w_out": w_out, "ln1_g": ln1_g, "ln1_b": ln1_b,
            "w_ffn_gate": w_ffn_gate, "w_ffn_up": w_ffn_up, "w_ffn_down": w_ffn_down,
            "ln2_g": ln2_g, "ln2_b": ln2_b}


def reference(x, positions, w_in, mu_shift, w_decay_up, w_decay_base, w_aaa_up, w_aaa_base, w_gate_up,
              k_k, k_a, r_k, lnx_g, lnx_b, q_norm_g, w_uq, kv_norm_g, w_ukv, w_proj_rwkv, w_proj_mla,
              w_out, ln1_g, ln1_b, w_ffn_gate, w_ffn_up, w_ffn_down, ln2_g, ln2_b):
    h = x
    for l in range(DEPTH):
        h = hybrid_layer(h, positions, w_in[l], mu_shift[l], w_decay_up[l], w_decay_base[l], w_aaa_up[l],
                         w_aaa_base[l], w_gate_up[l], k_k[l], k_a[l], r_k[l], lnx_g[l], lnx_b[l],
                         q_norm_g[l], w_uq[l], kv_norm_g[l], w_ukv[l], w_proj_rwkv[l], w_proj_mla[l],
                         w_out[l], ln1_g[l], ln1_b[l], w_ffn_gate[l], w_ffn_up[l], w_ffn_down[l],
                         ln2_g[l], ln2_b[l])
    return h
```

```python
import math
import os
from contextlib import ExitStack
import numpy as np
import concourse.bass as bass
import concourse.mybir as mybir
from concourse.bass_utils import run_bass_kernel_spmd

F32 = mybir.dt.float32
BF16 = mybir.dt.bfloat16
I32 = mybir.dt.int32
AF = mybir.ActivationFunctionType
ALU = mybir.AluOpType
AX = mybir.AxisListType

D = 1024
NH = 8
DIN = 4640
DFF = 2816
ALPHA = 2.0 ** 0.25
GN_EPS = 64e-5
LN_EPS = 1e-5
RMS_EPS = 1e-6
PI = math.pi
ENG = ['pe', 'act', 'dve', 'pool', 'sp']


class Buf:
    __slots__ = ('name', 'w', 'r')

    def __init__(self, name):
        self.name = name
        self.w = None
        self.r = {}


class Tl:
    def __init__(self, h, name):
        self.h = h
        self.b = Buf(name)

    def __getitem__(self, k):
        return self.h[k]


class Em:
    def __init__(self, nc, st, ndma=10):
        self.nc = nc
        self.q = {e: [] for e in ENG}
        self.sem = {}
        self.cnt = {}
        for e in ENG:
            self.sem[e] = st.enter_context(nc.semaphore('s_' + e))
            self.cnt[e] = 0
        self.dsem = {}
        for qn in ('sp', 'pool', 'act'):
            self.dsem[qn] = []
            for i in range(ndma):
                n = 'd_%s%d' % (qn, i)
                self.sem[n] = st.enter_context(nc.semaphore(n))
                self.cnt[n] = 0
                self.dsem[qn].append(n)
        self.drr = {qn: 0 for qn in self.dsem}
        self.waited = {e: {} for e in ENG}
        self.nbank = 0
        self.banks = []

    def _deps(self, eng, reads, writes, is_dma=False):
        need = {}

        def add(s, v):
            if need.get(s, 0) < v:
                need[s] = v
        for b in reads:
            if b.w is not None:
                add(*b.w)
        for b in writes:
            if b.w is not None:
                add(*b.w)
            for s, v in b.r.items():
                if s == eng and not is_dma:
                    continue
                add(s, v)
        if eng == 'pe' and not is_dma:
            need.pop('pe', None)
        waits = []
        wd = self.waited[eng]
        for s, v in need.items():
            if wd.get(s, 0) >= v:
                continue
            wd[s] = v
            waits.append((s, v))
        return waits

    def _mark(self, tok, reads, writes):
        s, v = tok
        for b in reads:
            if b.r.get(s, 0) < v:
                b.r[s] = v
        for b in writes:
            b.w = tok
            b.r = {}

    def op(self, eng, fn, r=(), w=(), inc=True):
        r = [x.b if isinstance(x, Tl) else x for x in r]
        w = [x.b if isinstance(x, Tl) else x for x in w]
        waits = self._deps(eng, r, w)
        if inc:
            self.cnt[eng] += 1
            tok = (eng, self.cnt[eng])
        else:
            tok = (eng, self.cnt[eng] + 1)
        self.q[eng].append((waits, fn, (eng, 1) if inc else None))
        self._mark(tok, r, w)

    def dma(self, qn, out, in_, r=(), w=()):
        r = [x.b if isinstance(x, Tl) else x for x in r]
        w = [x.b if isinstance(x, Tl) else x for x in w]
        s = self.dsem[qn][self.drr[qn]]
        self.drr[qn] = (self.drr[qn] + 1) % len(self.dsem[qn])
        waits = self._deps(qn, r, w, is_dma=True)
        wd = self.waited[qn]
        if self.cnt[s] > 0 and wd.get(s, 0) < self.cnt[s]:
            wd[s] = self.cnt[s]
            waits.append((s, self.cnt[s]))
        self.cnt[s] += 16
        tok = (s, self.cnt[s])
        self.q[qn].append((waits, (lambda e, o=out, i=in_: e.dma_start(out=o, in_=i)), (s, 16)))
        self._mark(tok, r, w)

    def barrier(self):
        for e in ENG:
            waits = []
            for s, c in self.cnt.items():
                if s == e or c == 0:
                    continue
                if self.waited[e].get(s, 0) < c:
                    self.waited[e][s] = c
                    waits.append((s, c))
            if waits:
                self.q[e].append((waits, None, None))

    def flush(self, st):
        self.barrier()
        block = st.enter_context(self.nc.Block())

        def mk(eng):
            items = self.q[eng]
            sem = self.sem

            def run(e):
                for waits, fn, inc in items:
                    for s, v in waits:
                        e.wait_ge(sem[s], v)
                    if fn is None:
                        continue
                    ins = fn(e)
                    if inc is not None:
                        ins.then_inc(sem[inc[0]], inc[1])
            return run
        block.tensor(mk('pe'))
        block.scalar(mk('act'))
        block.vector(mk('dve'))
        block.gpsimd(mk('pool'))
        block.sync(mk('sp'))
        self.q = {e: [] for e in ENG}

    def clear_all(self):
        self.nc.all_engine_barrier()
        for s in self.sem.values():
            self.nc.gpsimd.sem_clear(s)
        self.nc.all_engine_barrier()

    def bank(self):
        b = self.banks[self.nbank % len(self.banks)]
        self.nbank += 1
        return b


def host_consts():
    c = {}
    s = np.arange(128)[:, None]
    t = np.arange(128)[None, :]
    c['ident'] = np.eye(128, dtype=np.float32)
    msu = (s < t).astype(np.float32)
    miu = (s <= t).astype(np.float32)
    msl = (t < s).astype(np.float32)
    c['mask3'] = np.concatenate([msu, miu, msl], axis=1)
    c['tri'] = miu.copy()
    invf = (10000.0 ** (-np.arange(0, 32, 2, dtype=np.float32) / 32.0)).astype(np.float32)
    c['invf32'] = np.tile(np.concatenate([invf, invf])[None, :], (128, 1)).astype(np.float32)
    i96 = np.zeros((128, 1), np.float32)
    i96[0:16, 0] = invf
    i96[16:32, 0] = invf
    c['invf96'] = i96
    sel = np.zeros((32, 96), np.float32)
    sel[np.arange(32), np.arange(32)] = 1.0
    c['selk'] = sel
    am = np.zeros((128, 4, 512), np.float32)
    for j in range(4):
        am[:, j, :] = ((j * 128 + np.arange(128))[:, None] <= np.arange(512)[None, :])
    c['amask'] = am.reshape(128, 2048)
    return c


def build(S, NSEQ, dbg=False, stop=99, passes='AMBC'):
    NT = S // 128
    nc = bass.Bass("TRN2", target_bir_lowering=False)
    st = ExitStack()
    em = Em(nc, st)
    import os
    if os.environ.get('CLR0'):
        em.clear_all()

    def din(name, shape, dt=F32):
        return nc.dram_tensor(name, list(shape), dt, kind="ExternalInput").ap()

    x = din("x", [NSEQ, S, D])
    positions = din("positions", [NSEQ, S], I32)
    w_in = din("w_in", [D, DIN])
    mu_shift = din("mu_shift", [1792])
    w_decay_up = din("w_decay_up", [64, 512])
    w_decay_base = din("w_decay_base", [512])
    w_aaa_up = din("w_aaa_up", [64, 512])
    w_aaa_base = din("w_aaa_base", [512])
    w_gate_up = din("w_gate_up", [128, 512])
    k_k = din("k_k", [512])
    k_a = din("k_a", [512])
    r_k = din("r_k", [512])
    lnx_g = din("lnx_g", [512])
    lnx_b = din("lnx_b", [512])
    q_norm_g = din("q_norm_g", [512])
    w_uq = din("w_uq", [512, 768])
    kv_norm_g = din("kv_norm_g", [256])
    w_ukv = din("w_ukv", [256, 1024])
    if 'B' in passes:
        w_proj_rwkv = din("w_proj_rwkv", [512, D])
        w_proj_mla = din("w_proj_mla", [512, D])
        w_out = din("w_out", [D, D])
        ln1_g = din("ln1_g", [D])
        ln1_b = din("ln1_b", [D])
    if 'C' in passes:
        w_ffn_gate = din("w_ffn_gate", [D, DFF])
        w_ffn_up = din("w_ffn_up", [D, DFF])
        w_ffn_down = din("w_ffn_down", [DFF, D])
        ln2_g = din("ln2_g", [D])
        ln2_b = din("ln2_b", [D])
    hc = host_consts()
    cin = {k: din("c_" + k, v.shape) for k, v in hc.items()}
    out = nc.dram_tensor("out", [NSEQ, S, D], F32, kind="ExternalOutput").ap()

    okind = "ExternalOutput" if dbg else "Internal"

    def dscr(name, shape, dt):
        return nc.dram_tensor(name, list(shape), dt, kind=okind).ap()
    d_yrg = dscr("d_yrg", [NSEQ, S, 512], BF16)
    d_qt = dscr("d_qt", [NSEQ, 96, NH, S], BF16)
    d_kt = dscr("d_kt", [NSEQ, 96, NH, S], BF16)
    d_v = dscr("d_v", [NSEQ, S, NH * 65], BF16)
    d_gate = dscr("d_gate", [NSEQ, S, 2048], F32)
    d_ym = dscr("d_ym", [NSEQ, S, 512], BF16)
    d_h = dscr("d_h", [NSEQ, S, D], F32)
    db = {n: [[Buf("%s_%d_%d" % (n, s, i)) for i in range(NT)] for s in range(NSEQ)]
          for n in ('yrg', 'qt', 'kt', 'v', 'gate', 'ym', 'h')}

    for i in range(7):
        em.banks.append(Tl(nc.alloc_psum_tensor("bank%d" % i, [128, 512], F32), "bank%d" % i))
    fbank = Tl(nc.alloc_psum_tensor("fbank", [128, 512], F32), "fbank")
    fcol = [None]

    def fence():
        o = fcol[0]
        em.op('pe', lambda e: e.matmul(fbank.h[0:1, 0:1], lhsT=o[:, 0:1], rhs=o[:, 0:1], start=True, stop=True), r=[o], w=[fbank], inc=True)

    cur_stack = [None]

    def sb(name, shape, dt=F32):
        return Tl(cur_stack[0].enter_context(nc.sbuf_tensor(name, list(shape), dt)), name)

    def bcast_load(dst, src1d, n, q='sp'):
        em.dma(q, dst[:, 0:n], src1d.partition_broadcast(128), w=[dst])

    with ExitStack() as pst:
        cur_stack[0] = pst
        ident = sb("ident", [128, 128], BF16)
        mask3 = sb("mask3", [128, 384])
        tri = sb("tri", [128, 128], F32)
        ones1 = sb("ones1", [128, 1], F32)
        invf32 = sb("invf32", [128, 32], F32)
        invf96 = sb("invf96", [128, 1], F32)
        selk = sb("selk", [32, 96], BF16)
        zrow = sb("zrow", [1, 1792], F32)
        em.dma('pool', ident[:], cin['ident'], w=[ident])
        em.dma('sp', mask3[:], cin['mask3'], w=[mask3])
        em.dma('sp', tri[:], cin['tri'], w=[tri])
        em.dma('sp', invf32[:], cin['invf32'], w=[invf32])
        em.dma('sp', invf96[:], cin['invf96'], w=[invf96])
        em.dma('pool', selk[:], cin['selk'], w=[selk])
        em.op('dve', lambda e: e.memset(ones1[:], 1.0), w=[ones1])
        fcol[0] = ones1
        epsc = sb("epsc", [128, 4])
        em.op('dve', lambda e: e.memset(epsc[:, 0:1], GN_EPS), w=[epsc])
        em.op('dve', lambda e: e.memset(epsc[:, 1:2], RMS_EPS), r=[epsc], w=[epsc])
        em.op('dve', lambda e: e.memset(epsc[:, 2:3], LN_EPS), r=[epsc], w=[epsc])
        em.op('dve', lambda e: e.memset(epsc[:, 3:4], -PI), r=[epsc], w=[epsc])
        em.op('dve', lambda e: e.memset(zrow[:], 0.0), w=[zrow])

        Win = sb("Win", [128, 8, 2624], BF16)
        wv = w_in.rearrange("(c p) n -> p c n", p=128)
        for c in range(8):
            em.dma('pool', Win[:, c, 0:2592], wv[:, c, 0:2592], w=[Win])
        em.op('act', lambda e: e.mul(out=Win[:, :, 2592:2608], in_=Win[:, :, 2576:2592], mul=-1.0), r=[Win], w=[Win])
        em.op('act', lambda e: e.copy(out=Win[:, :, 2608:2624], in_=Win[:, :, 2560:2576]), r=[Win], w=[Win])

        Wq = sb("Wq", [128, 4, NH, 96], BF16)
        Wqr = sb("Wqr", [128, 4, NH, 32], BF16)
        qv = w_uq.rearrange("(c p) (h d) -> p c h d", p=128, h=NH)
        for c in range(4):
            em.dma('pool', Wq[:, c, :, 0:32], qv[:, c, :, 64:96], w=[Wq])
            em.dma('pool', Wq[:, c, :, 32:96], qv[:, c, :, 0:64], w=[Wq])
        em.op('act', lambda e: e.mul(out=Wqr[:, :, :, 0:16], in_=Wq[:, :, :, 16:32], mul=-1.0), r=[Wq, Wqr], w=[Wqr])
        em.op('act', lambda e: e.copy(out=Wqr[:, :, :, 16:32], in_=Wq[:, :, :, 0:16]), r=[Wq, Wqr], w=[Wqr])
        WkN = sb("WkN", [128, 2, NH, 96], BF16)
        WkV = sb("WkV", [128, 2, NH, 64], BF16)
        kvv = w_ukv.rearrange("(c p) (h e) -> p c h e", p=128, h=NH)
        em.op('dve', lambda e: e.memset(WkN[:], 0.0), w=[WkN])
        for c in range(2):
            em.dma('pool', WkN[:, c, :, 32:96], kvv[:, c, :, 0:64], r=[WkN], w=[WkN])
            em.dma('pool', WkV[:, c, :, :], kvv[:, c, :, 64:128], w=[WkV])
        Wlo = sb("Wlo", [128, 512], BF16)
        Wg = sb("Wg", [128, 512], BF16)
        em.dma('pool', Wlo[0:64, :], w_decay_up, w=[Wlo])
        em.dma('pool', Wlo[64:128, :], w_aaa_up, w=[Wlo])
        em.dma('pool', Wg[:], w_gate_up, w=[Wg])

        mu_bc = sb("mu_bc", [128, 1792])
        bcast_load(mu_bc, mu_shift, 1792)
        pbc = {}
        for nm, src in (('kk', k_k), ('ka', k_a), ('rk', r_k), ('lg', lnx_g), ('lb', lnx_b),
                        ('bw', w_decay_base), ('ba', w_aaa_base), ('qg', q_norm_g)):
            pbc[nm] = sb("bc_" + nm, [128, 512])
            bcast_load(pbc[nm], src, 512)
        kvg = sb("bc_kvg", [128, 256])
        bcast_load(kvg, kv_norm_g, 256)

        xb = [sb("xb%d" % i, [128, 1024], BF16) for i in range(2)]
        xT = sb("xT", [128, 8, 128], BF16)
        zc = sb("zc", [128, 1792])
        zlast = sb("zlast", [1, 1792])
        zs = sb("zs", [128, 1792])
        loin = sb("loin", [128, 256], BF16)
        loT = sb("loT", [128, 2, 128], BF16)
        t_a = sb("t_a", [128, 512])
        t_lw = sb("t_lw", [128, 512])
        t_g = sb("t_g", [128, 512])
        t_kk = sb("t_kk", [128, 512])
        t_k = sb("t_k", [128, 512])
        t_tmp = sb("t_tmp", [128, 512])
        t_tmp2 = sb("t_tmp2", [128, 512])
        t_e1 = sb("t_e1", [128, 512])
        t_e2 = sb("t_e2", [128, 512])
        t_e3 = sb("t_e3", [128, 512])
        st8 = sb("st8", [128, 8, 8])
        tmA = sb("tmA", [128, 512], BF16)
        tmR = sb("tmR", [128, 512], BF16)
        tmB = sb("tmB", [128, 512], BF16)
        tmK = sb("tmK", [128, 512], BF16)
        Vb = sb("Vb", [128, 512], BF16)
        AT = sb("AT", [128, 4, 512], BF16)
        gC = sb("gC", [128, 4])
        M12 = sb("M12", [128, NH, 512], BF16)
        Xs = sb("Xs", [128, NH, 128], BF16)
        XTs = sb("XTs", [128, NH, 128], BF16)
        TT = sb("TT", [128, NH, 128], BF16)
        Hf = sb("Hf", [128, 4, 64])
        Hb = sb("Hb", [128, 4, 64], BF16)
        Htmp = sb("Htmp", [128, 4, 64])
        RHSb = sb("RHSb", [128, NH, 64], BF16)
        Ub = sb("Ub", [128, NH, 64], BF16)
        yrg = [sb("yrg%d" % i, [128, 512], BF16) for i in range(2)]
        posi = sb("posi", [128, 1], I32)
        posf = sb("posf", [128, 1])
        posbi = sb("posbi", [96, 128], I32)
        angt = sb("angt", [128, 64])
        cst = sb("cst", [96, 256])
        csti = sb("csti", [96, 256], I32)
        cstf = sb("cstf", [96, 256])
        angti = sb("angti", [128, 64], I32)
        angtf = sb("angtf", [128, 64])
        angf = sb("angf", [96, 128])
        cqn = sb("cqn", [128, 512], BF16)
        ckvn = sb("ckvn", [128, 256], BF16)
        kpef = sb("kpef", [128, 32], BF16)
        kpt = sb("kpt", [128, 64])
        cT = sb("cT", [128, 7, 128], BF16)
        sst = sb("sst", [128, 4])
        qtb = sb("qtb", [96, NH, 128], BF16)
        ktb = sb("ktb", [96, NH, 128], BF16)
        vtb = sb("vtb", [128, NH, 65], BF16)
        em.op('dve', lambda e: e.memset(vtb[:], 1.0), w=[vtb])

        H8 = lambda ap: ap.rearrange("p (h d) -> p h d", h=NH)
        dumped = set()

        def dump(name, tl, shape, dt=F32, view=None):
            import os
            if not dbg or name in dumped or os.environ.get('NODUMP'):
                return
            dumped.add(name)
            dd = nc.dram_tensor("g_" + name, list(shape), dt, kind="ExternalOutput").ap()
            em.dma('sp', dd, (tl[:] if view is None else view), r=[tl], w=[Buf("g_" + name)])

        def rr_sin(T, y, yi, yf, Ti, Tf):
            em.op('dve', lambda e: e.tensor_copy(out=yi, in_=y), r=[T], w=[Ti])
            em.op('dve', lambda e: e.tensor_copy(out=yf, in_=yi), r=[Ti], w=[Tf])
            em.op('dve', lambda e: e.tensor_tensor(out=y, in0=y, in1=yf, op=ALU.subtract), r=[T, Tf], w=[T])
            em.op('dve', lambda e: e.tensor_scalar(out=yf, in0=y, scalar1=0.5, scalar2=-1.0, op0=ALU.is_gt, op1=ALU.mult), r=[T], w=[Tf])
            em.op('dve', lambda e: e.tensor_tensor(out=y, in0=y, in1=yf, op=ALU.add), r=[T, Tf], w=[T])
            em.op('dve', lambda e: e.tensor_scalar(out=yf, in0=y, scalar1=-0.5, scalar2=None, op0=ALU.is_lt), r=[T], w=[Tf])
            em.op('dve', lambda e: e.tensor_tensor(out=y, in0=y, in1=yf, op=ALU.add), r=[T, Tf], w=[T])
            em.op('dve', lambda e: e.tensor_scalar(out=y, in0=y, scalar1=-0.5, scalar2=0.5, op0=ALU.max, op1=ALU.min), r=[T], w=[T])
            em.op('act', lambda e: e.activation(out=y, in_=y, func=AF.Sin, scale=2 * PI), r=[T], w=[T])

        def load_x(s, i):
            em.dma('pool', xb[i % 2][:], x[s, i * 128:(i + 1) * 128, :], w=[xb[i % 2]])

        gn = [0]

        def pass_a_tile(s, i):
            par = i % 2
            xbt = xb[par]
            bk = em.bank()
            pt = bk.h[:, :].bitcast(BF16)
            for c in range(8):
                em.op('pe', lambda e, pt=pt, c=c: e.transpose(out=pt[:, c * 128:(c + 1) * 128], in_=xbt[:, c * 128:(c + 1) * 128], identity=ident[:]),
                      r=[xbt, ident], w=[bk], inc=(c == 7))
            em.op('act', lambda e, pt=pt: e.copy(out=xT[:].rearrange("p c t -> p (c t)"), in_=pt), r=[bk], w=[xT])
            dump("xT", xT, [128, 8, 128], BF16)
            dump("xb", xbt, [128, 1024], BF16)
            if i + 1 < NT:
                load_x(s, i + 1)
            elif s + 1 < NSEQ:
                load_x(s + 1, 0)

            def proj(c0, n):
                b = em.bank()
                for c in range(8):
                    em.op('pe', lambda e, c=c, b=b: e.matmul(b.h[:, 0:n], lhsT=xT[:, c, :], rhs=Win[:, c, c0:c0 + n], start=(c == 0), stop=(c == 7)),
                          r=[xT, Win], w=[b], inc=(c == 7))
                return b
            zct = zc
            for g in range(4):
                n = 512 if g < 3 else 256
                b = proj(g * 512, n)
                em.op('act', lambda e, b=b, g=g, n=n: e.copy(out=zct[:, g * 512:g * 512 + n], in_=b.h[:, 0:n]), r=[b], w=[zct])
            dump("zc", zct, [128, 1792])
            dump("Win0", Win, [128, 512], BF16, view=Win[:, 0, 0:512])
            dump("Win7", Win, [128, 512], BF16, view=Win[:, 7, 2112:2624])
            em.dma('sp', zs[1:128, :], zct[0:127, :], r=[zct], w=[zs])
            if i == 0:
                em.dma('sp', zs[0:1, :], zrow[:], r=[zrow], w=[zs])
            else:
                em.dma('sp', zs[0:1, :], zlast[:], r=[zlast], w=[zs])
            em.dma('sp', zlast[:], zct[127:128, :], r=[zct], w=[zlast])
            em.op('pool', lambda e: e.tensor_tensor(out=zs[:], in0=zs[:], in1=zct[:], op=ALU.subtract), r=[zs, zct], w=[zs])
            em.op('dve', lambda e: e.tensor_tensor(out=zs[:], in0=zs[:], in1=mu_bc[:], op=ALU.mult), r=[zs, mu_bc], w=[zs])
            em.op('pool', lambda e: e.tensor_tensor(out=zs[:], in0=zs[:], in1=zct[:], op=ALU.add), r=[zs, zct], w=[zs])
            zr = zs[:, 0:512]
            zk = zs[:, 512:1024]
            zv = zs[:, 1024:1536]
            if stop <= 1:
                return
            em.op('act', lambda e: e.activation(out=loin[:, 0:64], in_=zs[:, 1536:1600], func=AF.Tanh), r=[zs], w=[loin])
            em.op('act', lambda e: e.activation(out=loin[:, 128:256], in_=zs[:, 1664:1792], func=AF.Sigmoid), r=[zs], w=[loin])
            em.op('act', lambda e: e.copy(out=loin[:, 64:128], in_=zs[:, 1600:1664]), r=[zs], w=[loin])
            if stop <= 1.2:
                return
            bk = em.bank()
            pt = bk.h[:, :].bitcast(BF16)
            for c in range(2):
                em.op('pe', lambda e, pt=pt, c=c: e.transpose(out=pt[:, c * 128:(c + 1) * 128], in_=loin[:, c * 128:(c + 1) * 128], identity=ident[:]),
                      r=[loin, ident], w=[bk], inc=(c == 1))
            em.op('dve', lambda e, pt=pt: e.tensor_copy(out=loT[:].rearrange("p c t -> p (c t)"), in_=pt[:, 0:256]), r=[bk], w=[loT])
            if stop <= 1.4:
                return
            b_w = em.bank()
            em.op('pe', lambda e: e.matmul(b_w.h[:, :], lhsT=loT[0:64, 0, :], rhs=Wlo[0:64, :], start=True, stop=True), r=[loT, Wlo], w=[b_w], inc=False)
            b_a = em.bank()
            em.op('pe', lambda e: e.matmul(b_a.h[:, :], lhsT=loT[64:128, 0, :], rhs=Wlo[64:128, :], start=True, stop=True), r=[loT, Wlo], w=[b_a], inc=False)
            b_g = em.bank()
            em.op('pe', lambda e: e.matmul(b_g.h[:, :], lhsT=loT[:, 1, :], rhs=Wg[:, :], start=True, stop=True), r=[loT, Wg], w=[b_g])
            if stop <= 1.6:
                return
            em.op('dve', lambda e: e.tensor_tensor(out=t_lw[:], in0=b_w.h[:, :], in1=pbc['bw'][:], op=ALU.add), r=[b_w, pbc['bw']], w=[t_lw])
            if stop <= 1.7:
                return
            em.op('act', lambda e: e.activation(out=t_lw[:], in_=t_lw[:], func=AF.Sigmoid), r=[t_lw], w=[t_lw])
            if stop <= 1.8:
                return
            em.op('dve', lambda e: e.tensor_scalar(out=t_lw[:], in0=t_lw[:], scalar1=-math.exp(-0.5), scalar2=None, op0=ALU.mult), r=[t_lw], w=[t_lw])
            em.op('dve', lambda e: e.tensor_tensor(out=t_a[:], in0=b_a.h[:, :], in1=pbc['ba'][:], op=ALU.add), r=[b_a, pbc['ba']], w=[t_a])
            em.op('act', lambda e: e.activation(out=t_a[:], in_=t_a[:], func=AF.Sigmoid), r=[t_a], w=[t_a])
            if stop <= 1.9:
                return
            em.op('act', lambda e: e.copy(out=t_g[:], in_=b_g.h[:, :]), r=[b_g], w=[t_g])
            if stop <= 2:
                return
            em.op('pool', lambda e: e.tensor_tensor(out=t_kk[:], in0=zk, in1=pbc['kk'][:], op=ALU.mult), r=[zs, pbc['kk']], w=[t_kk])
            em.op('pool', lambda e: e.tensor_tensor(out=t_tmp[:], in0=t_kk[:], in1=t_kk[:], op=ALU.mult), r=[t_kk], w=[t_tmp])
            em.op('dve', lambda e: e.tensor_reduce(out=st8[:, 0, :], in_=H8(t_tmp[:]), axis=AX.X, op=ALU.add), r=[t_tmp], w=[st8])
            em.op('act', lambda e: e.activation(out=st8[:, 1, :], in_=st8[:, 0, :], func=AF.Sqrt), r=[st8], w=[st8])
            em.op('dve', lambda e: e.tensor_scalar(out=st8[:, 1, :], in0=st8[:, 1, :], scalar1=1e-12, scalar2=None, op0=ALU.max), r=[st8], w=[st8])
            em.op('dve', lambda e: e.reciprocal(out=st8[:, 2, :], in_=st8[:, 1, :]), r=[st8], w=[st8])
            em.op('dve', lambda e: e.tensor_tensor(out=H8(t_kk[:]), in0=H8(t_kk[:]), in1=st8[:, 2, :].unsqueeze(2).to_broadcast([128, NH, 64]), op=ALU.mult),
                  r=[t_kk, st8], w=[t_kk])
            em.op('dve', lambda e: e.scalar_tensor_tensor(out=t_k[:], in0=t_a[:], scalar=-1.0, in1=pbc['ka'][:], op0=ALU.add, op1=ALU.mult), r=[t_a, pbc['ka']], w=[t_k])
            em.op('dve', lambda e: e.scalar_tensor_tensor(out=t_k[:], in0=t_k[:], scalar=1.0, in1=zk, op0=ALU.add, op1=ALU.mult), r=[t_k, zs], w=[t_k])
            em.op('pool', lambda e: e.tensor_tensor(out=t_tmp[:], in0=zr, in1=t_k[:], op=ALU.mult), r=[zs, t_k, st8], w=[t_tmp])
            em.op('pool', lambda e: e.tensor_tensor(out=t_tmp[:], in0=t_tmp[:], in1=pbc['rk'][:], op=ALU.mult), r=[t_tmp, pbc['rk']], w=[t_tmp])
            em.op('dve', lambda e: e.tensor_reduce(out=st8[:, 3, :], in_=H8(t_tmp[:]), axis=AX.X, op=ALU.add), r=[t_tmp], w=[st8])
            b_c = em.bank()
            em.op('pe', lambda e: e.matmul(b_c.h[:, :], lhsT=tri[:, :], rhs=t_lw[:, :], start=True, stop=True), r=[tri, t_lw], w=[b_c])
            b_gc = em.bank()
            for p in range(4):
                em.op('pe', lambda e, p=p: e.matmul(b_gc.h[:, p:p + 1], lhsT=t_lw[:, p * 128:(p + 1) * 128], rhs=ones1[:, :], start=True, stop=True),
                      r=[t_lw, ones1], w=[b_gc], inc=(p == 3))
            em.op('act', lambda e: e.activation(out=gC[:], in_=b_gc.h[:, 0:4], func=AF.Exp), r=[b_gc], w=[gC])
            em.op('act', lambda e: e.activation(out=t_e1[:], in_=b_c.h[:, :], func=AF.Exp), r=[b_c], w=[t_e1])
            em.op('act', lambda e: e.activation(out=t_e2[:], in_=b_c.h[:, :], func=AF.Exp, scale=-1.0), r=[b_c], w=[t_e2])
            em.op('dve', lambda e: e.tensor_tensor(out=t_e3[:], in0=b_c.h[:, :], in1=t_lw[:], op=ALU.subtract), r=[b_c, t_lw], w=[t_e3, b_c])
            em.op('act', lambda e: e.activation(out=t_e3[:], in_=t_e3[:], func=AF.Exp), r=[t_e3], w=[t_e3])
            em.op('pool', lambda e: e.tensor_tensor(out=tmR[:], in0=zr, in1=t_e1[:], op=ALU.mult), r=[zs, t_e1], w=[tmR])
            em.op('pool', lambda e: e.tensor_tensor(out=tmK[:], in0=t_k[:], in1=t_e2[:], op=ALU.mult), r=[t_k, t_e2], w=[tmK])
            em.op('dve', lambda e: e.tensor_tensor(out=t_tmp2[:], in0=t_kk[:], in1=t_a[:], op=ALU.mult), r=[t_kk, t_a], w=[t_tmp2])
            em.op('dve', lambda e: e.tensor_tensor(out=tmB[:], in0=t_tmp2[:], in1=t_e2[:], op=ALU.mult), r=[t_tmp2, t_e2], w=[tmB])
            em.op('dve', lambda e: e.scalar_tensor_tensor(out=tmA[:], in0=t_kk[:], scalar=-1.0, in1=t_e3[:], op0=ALU.mult, op1=ALU.mult), r=[t_kk, t_e3], w=[tmA])
            em.op('act', lambda e: e.copy(out=Vb[:], in_=zv), r=[zs], w=[Vb])
            if stop <= 3:
                return
            dump("zs", zs, [128, 1792])
            dump("lw", t_lw, [128, 512])
            dump("a", t_a, [128, 512])
            dump("kk", t_kk, [128, 512])
            dump("k", t_k, [128, 512])
            dump("e1", t_e1, [128, 512])
            dump("e2", t_e2, [128, 512])
            dump("e3", t_e3, [128, 512])
            dump("tmA", tmA, [128, 512], BF16)
            dump("tmB", tmB, [128, 512], BF16)
            dump("gC", gC, [128, 4])
            for half in range(2):
                bk = em.bank()
                pt = bk.h[:, :].bitcast(BF16)
                n = 0
                for pp in range(2):
                    p = half * 2 + pp
                    for j, src in enumerate((tmA, tmR, tmB, tmK)):
                        n += 1
                        em.op('pe', lambda e, pt=pt, p=p, pp=pp, j=j, src=src: e.transpose(out=pt[:, pp * 512 + j * 128: pp * 512 + (j + 1) * 128],
                                                                                  in_=src[:, p * 128:(p + 1) * 128], identity=ident[:]),
                              r=[src, ident], w=[bk], inc=(n == 8))
                eng = 'dve'
                if eng == 'act':
                    em.op('act', lambda e, half=half, pt=pt: e.copy(out=AT[:, half * 2:half * 2 + 2, :].rearrange("p a b -> p (a b)"), in_=pt), r=[bk], w=[AT])
                else:
                    em.op('dve', lambda e, half=half, pt=pt: e.tensor_copy(out=AT[:, half * 2:half * 2 + 2, :].rearrange("p a b -> p (a b)"), in_=pt), r=[bk], w=[AT])
            if stop <= 3.2:
                return
            msk2 = mask3[:, 0:256].unsqueeze(1).to_broadcast([128, 2, 256])
            for hp in range(4):
                bb = []
                for q in range(2):
                    h = 2 * hp + q
                    ps = slice(q * 64, (q + 1) * 64)
                    b1 = em.bank()
                    bb.append((h, b1))
                    em.op('pe', lambda e, ps=ps, b1=b1, hp=hp: e.matmul(b1.h[:, 0:256], lhsT=AT[ps, hp, 256:384], rhs=AT[ps, hp, 0:256], start=True, stop=True),
                          r=[AT], w=[b1], inc=False)
                    em.op('pe', lambda e, ps=ps, b1=b1, hp=hp: e.matmul(b1.h[:, 256:512], lhsT=AT[ps, hp, 384:512], rhs=AT[ps, hp, 0:256], start=True, stop=True),
                          r=[AT], w=[b1], inc=False)
                fence()
                for h, b1 in bb:
                    em.op('dve', lambda e, b1=b1, h=h: e.tensor_tensor(out=M12[:, h, :].rearrange("p (a b) -> p a b", a=2), in0=b1.h[:, :].rearrange("p (a b) -> p a b", a=2), in1=msk2, op=ALU.mult),
                          r=[b1, mask3], w=[M12])
            if stop <= 3.4:
                return
            msl4 = mask3[:, 256:384].unsqueeze(1).to_broadcast([128, 4, 128])
            X0, XT0 = Xs, XTs
            for q in range(2):
                b3 = em.bank()
                ps = slice(q * 64, (q + 1) * 64)
                for hh in range(4):
                    h = 2 * hh + q
                    em.op('pe', lambda e, h=h, hh=hh, ps=ps, b3=b3: e.matmul(b3.h[:, hh * 128:(hh + 1) * 128], lhsT=AT[ps, h // 2, 0:128], rhs=AT[ps, h // 2, 256:384], start=True, stop=True),
                          r=[AT], w=[b3], inc=False)
                fence()
                em.op('dve', lambda e, b3=b3, q=q: e.tensor_tensor(out=X0[:, q:NH:2, :], in0=b3.h[:, :].rearrange("p (a b) -> p a b", a=4), in1=msl4, op=ALU.mult),
                      r=[b3, mask3], w=[X0])
            if stop <= 3.6:
                return
            em.op('pool', lambda e: e.tensor_copy(out=XT0[:], in_=M12[:, :, 0:128]), r=[M12], w=[XT0])
            em.op('pool', lambda e: e.tensor_tensor(out=TT[:], in0=M12[:, :, 0:128], in1=ident[:].unsqueeze(1).to_broadcast([128, NH, 128]), op=ALU.add),
                  r=[M12, ident], w=[TT])
            if stop <= 4:
                return
            dump("AT", AT, [128, 4, 512], BF16)
            dump("M12", M12, [128, NH, 512], BF16)
            dump("X0", Xs, [128, NH, 128], BF16)
            dump("TT0", TT, [128, NH, 128], BF16)
            for lvl in range(1, 7):
                bxs, bxts, bts = [], [], []
                for g4 in range(2):
                    bx = em.bank()
                    bxs.append(bx)
                    for hh in range(4):
                        h = g4 * 4 + hh
                        em.op('pe', lambda e, h=h, hh=hh, bx=bx: e.matmul(bx.h[:, hh * 128:(hh + 1) * 128], lhsT=XTs[:, h, :], rhs=Xs[:, h, :], start=True, stop=True),
                              r=[Xs, XTs], w=[bx], inc=(hh == 3))
                    if lvl < 6:
                        bxt = em.bank()
                        bxts.append(bxt)
                        for hh in range(4):
                            h = g4 * 4 + hh
                            em.op('pe', lambda e, h=h, hh=hh, bxt=bxt: e.matmul(bxt.h[:, hh * 128:(hh + 1) * 128], lhsT=Xs[:, h, :], rhs=XTs[:, h, :], start=True, stop=True),
                                  r=[Xs, XTs], w=[bxt], inc=(hh == 3))
                for g4 in range(2):
                    hs = slice(g4 * 4, g4 * 4 + 4)
                    em.op('act', lambda e, bx=bxs[g4], hs=hs: e.copy(out=Xs[:, hs, :], in_=bx.h[:, :].rearrange("p (a b) -> p a b", a=4)), r=[bxs[g4]], w=[Xs])
                    if lvl < 6:
                        em.op('dve', lambda e, bxt=bxts[g4], hs=hs: e.tensor_copy(out=XTs[:, hs, :], in_=bxt.h[:, :].rearrange("p (a b) -> p a b", a=4)), r=[bxts[g4]], w=[XTs])
                for g4 in range(2):
                    bt = em.bank()
                    bts.append(bt)
                    for hh in range(4):
                        h = g4 * 4 + hh
                        em.op('pe', lambda e, h=h, hh=hh, bt=bt: e.matmul(bt.h[:, hh * 128:(hh + 1) * 128], lhsT=Xs[:, h, :], rhs=TT[:, h, :], start=True, stop=True),
                              r=[Xs, TT], w=[bt], inc=(hh == 3))
                for g4 in range(2):
                    hs = slice(g4 * 4, g4 * 4 + 4)
                    em.op('dve', lambda e, bt=bts[g4], hs=hs: e.tensor_tensor(out=TT[:, hs, :], in0=bt.h[:, :].rearrange("p (a b) -> p a b", a=4), in1=TT[:, hs, :], op=ALU.add),
                          r=[bts[g4], TT], w=[TT])
            dump("TT", TT, [128, NH, 128], BF16)
            if i == 0:
                em.op('dve', lambda e: e.memset(Hf[:], 0.0), w=[Hf])
                em.op('dve', lambda e: e.memset(Hb[:], 0.0), w=[Hb])
            bR = em.bank()
            for h in range(NH):
                ps = slice((h % 2) * 64, (h % 2) * 64 + 64)
                em.op('pe', lambda e, h=h, ps=ps: e.matmul(bR.h[:, h * 64:(h + 1) * 64], lhsT=AT[ps, h // 2, 0:128], rhs=Hb[ps, h // 2, :], start=True, stop=False),
                      r=[AT, Hb], w=[bR], inc=False)
                em.op('pe', lambda e, h=h: e.matmul(bR.h[:, h * 64:(h + 1) * 64], lhsT=M12[:, h, 256:384], rhs=Vb[:, h * 64:(h + 1) * 64], start=False, stop=True),
                      r=[M12, Vb], w=[bR], inc=(h == NH - 1))
            em.op('act', lambda e: e.copy(out=RHSb[:].rearrange("p h d -> p (h d)"), in_=bR.h[:, :]), r=[bR], w=[RHSb])
            bU = em.bank()
            for h in range(NH):
                em.op('pe', lambda e, h=h: e.matmul(bU.h[:, h * 64:(h + 1) * 64], lhsT=TT[:, h, :], rhs=RHSb[:, h, :], start=True, stop=True),
                      r=[TT, RHSb], w=[bU], inc=(h == NH - 1))
            em.op('act', lambda e: e.copy(out=Ub[:].rearrange("p h d -> p (h d)"), in_=bU.h[:, :]), r=[bU], w=[Ub])
            bY = em.bank()
            for h in range(NH):
                ps = slice((h % 2) * 64, (h % 2) * 64 + 64)
                hsl = slice(h * 64, (h + 1) * 64)
                em.op('pe', lambda e, h=h, ps=ps, hsl=hsl: e.matmul(bY.h[:, hsl], lhsT=AT[ps, h // 2, 128:256], rhs=Hb[ps, h // 2, :], start=True, stop=False),
                      r=[AT, Hb], w=[bY], inc=False)
                em.op('pe', lambda e, h=h, hsl=hsl: e.matmul(bY.h[:, hsl], lhsT=M12[:, h, 128:256], rhs=Ub[:, h, :], start=False, stop=False),
                      r=[M12, Ub], w=[bY], inc=False)
                em.op('pe', lambda e, h=h, hsl=hsl: e.matmul(bY.h[:, hsl], lhsT=M12[:, h, 384:512], rhs=Vb[:, hsl], start=False, stop=True),
                      r=[M12, Vb], w=[bY], inc=(h == NH - 1))
            bH = em.bank()
            for p in range(4):
                psl = slice(p * 128, (p + 1) * 128)
                em.op('pe', lambda e, p=p, psl=psl: e.matmul(bH.h[:, psl], lhsT=tmB[:, psl], rhs=Ub[:, 2 * p:2 * p + 2, :].rearrange("p a b -> p (a b)"), start=True, stop=False),
                      r=[tmB, Ub], w=[bH], inc=False)
                em.op('pe', lambda e, p=p, psl=psl: e.matmul(bH.h[:, psl], lhsT=tmK[:, psl], rhs=Vb[:, psl], start=False, stop=True),
                      r=[tmK, Vb], w=[bH], inc=(p == 3))
            bHv = bH.h[:, :].rearrange("p (a b) -> p a b", a=4)
            em.op('dve', lambda e: e.tensor_tensor(out=Htmp[0:64, :, :], in0=bHv[0:64, :, 0:64], in1=Hf[0:64, :, :], op=ALU.add), r=[bH, Hf], w=[Htmp])
            em.op('dve', lambda e: e.tensor_tensor(out=Htmp[64:128, :, :], in0=bHv[64:128, :, 64:128], in1=Hf[64:128, :, :], op=ALU.add), r=[bH, Hf], w=[Htmp])
            gcb = gC[:, :].unsqueeze(2).to_broadcast([128, 4, 64])
            em.op('dve', lambda e: e.tensor_tensor(out=Hf[:], in0=Htmp[:], in1=gcb, op=ALU.mult), r=[Htmp, gC], w=[Hf])
            em.op('pool', lambda e: e.tensor_copy(out=Hb[:], in_=Hf[:]), r=[Hf], w=[Hb])
            if stop <= 6:
                return
            dump("RHSb", RHSb, [128, NH, 64], BF16)
            dump("Ub", Ub, [128, NH, 64], BF16)
            dump("Hf", Hf, [128, 4, 64])
            yv = H8(bY.h[:, :])
            em.op('act', lambda e: e.copy(out=t_e1[:], in_=bY.h[:, :]), r=[bY], w=[t_e1])
            dump('y', t_e1, [128, 512])
            em.op('dve', lambda e: e.tensor_reduce(out=st8[:, 4, :], in_=H8(t_e1[:]), axis=AX.X, op=ALU.add), r=[t_e1], w=[st8])
            em.op('pool', lambda e: e.tensor_tensor(out=t_e2[:], in0=t_e1[:], in1=t_e1[:], op=ALU.mult), r=[t_e1], w=[t_e2])
            em.op('dve', lambda e: e.tensor_reduce(out=st8[:, 5, :], in_=H8(t_e2[:]), axis=AX.X, op=ALU.add), r=[t_e2], w=[st8])
            em.op('dve', lambda e: e.tensor_scalar(out=st8[:, 4, :], in0=st8[:, 4, :], scalar1=1.0 / 64, scalar2=None, op0=ALU.mult), r=[st8], w=[st8])
            em.op('dve', lambda e: e.tensor_tensor(out=st8[:, 6, :], in0=st8[:, 4, :], in1=st8[:, 4, :], op=ALU.mult), r=[st8], w=[st8])
            em.op('dve', lambda e: e.scalar_tensor_tensor(out=st8[:, 5, :], in0=st8[:, 5, :], scalar=1.0 / 64, in1=st8[:, 6, :], op0=ALU.mult, op1=ALU.subtract), r=[st8], w=[st8])
            em.op('act', lambda e: e.activation(out=st8[:, 5, :], in_=st8[:, 5, :], func=AF.Sqrt, bias=epsc[:, 0:1], scale=1.0), r=[st8, epsc], w=[st8])
            em.op('dve', lambda e: e.reciprocal(out=st8[:, 5, :], in_=st8[:, 5, :]), r=[st8], w=[st8])
            em.op('dve', lambda e: e.tensor_tensor(out=H8(t_e1[:]), in0=H8(t_e1[:]), in1=st8[:, 4, :].unsqueeze(2).to_broadcast([128, NH, 64]), op=ALU.subtract), r=[t_e1, st8], w=[t_e1])
            em.op('dve', lambda e: e.tensor_tensor(out=H8(t_e1[:]), in0=H8(t_e1[:]), in1=st8[:, 5, :].unsqueeze(2).to_broadcast([128, NH, 64]), op=ALU.mult), r=[t_e1, st8], w=[t_e1])
            em.op('pool', lambda e: e.tensor_tensor(out=t_e1[:], in0=t_e1[:], in1=pbc['lg'][:], op=ALU.mult), r=[t_e1, pbc['lg']], w=[t_e1])
            em.op('pool', lambda e: e.tensor_tensor(out=t_e1[:], in0=t_e1[:], in1=pbc['lb'][:], op=ALU.add), r=[t_e1, pbc['lb']], w=[t_e1])
            em.op('dve', lambda e: e.tensor_tensor(out=H8(t_e2[:]), in0=H8(zv), in1=st8[:, 3, :].unsqueeze(2).to_broadcast([128, NH, 64]), op=ALU.mult), r=[zs, st8], w=[t_e2])
            em.op('pool', lambda e: e.tensor_tensor(out=t_e1[:], in0=t_e1[:], in1=t_e2[:], op=ALU.add), r=[t_e1, t_e2], w=[t_e1])
            yo = yrg[par]
            em.op('pool', lambda e: e.tensor_tensor(out=yo[:], in0=t_e1[:], in1=t_g[:], op=ALU.mult), r=[t_e1, t_g], w=[yo])
            em.dma('pool', d_yrg[s, i * 128:(i + 1) * 128, :], yo[:], r=[yo], w=[db['yrg'][s][i]])

            if stop <= 7:
                return
            em.dma('sp', posi[:], positions[s, i * 128:(i + 1) * 128].unsqueeze(1), w=[posi])
            em.dma('sp', posbi[:], positions[s, i * 128:(i + 1) * 128].partition_broadcast(96), w=[posbi])
            em.op('dve', lambda e: e.tensor_copy(out=posf[:], in_=posi[:]), r=[posi], w=[posf])
            bq = proj(1792, 512)
            bkv = proj(2304, 320)
            em.op('act', lambda e: e.activation(out=t_tmp[:], in_=bq.h[:, :], func=AF.Square, accum_out=sst[:, 0:1]), r=[bq], w=[t_tmp, sst])
            em.op('act', lambda e: e.activation(out=t_tmp[:, 0:256], in_=bkv.h[:, 0:256], func=AF.Square, accum_out=sst[:, 1:2]), r=[bkv, sst], w=[t_tmp, sst])
            em.op('act', lambda e: e.activation(out=sst[:, 2:3], in_=sst[:, 0:1], func=AF.Sqrt, bias=epsc[:, 1:2], scale=1.0 / 512), r=[sst, epsc], w=[sst])
            em.op('act', lambda e: e.activation(out=sst[:, 3:4], in_=sst[:, 1:2], func=AF.Sqrt, bias=epsc[:, 1:2], scale=1.0 / 256), r=[sst, epsc], w=[sst])
            em.op('dve', lambda e: e.reciprocal(out=sst[:, 2:4], in_=sst[:, 2:4]), r=[sst], w=[sst])
            em.op('dve', lambda e: e.scalar_tensor_tensor(out=cqn[:], in0=bq.h[:, :], scalar=sst[:, 2:3], in1=pbc['qg'][:], op0=ALU.mult, op1=ALU.mult), r=[bq, sst, pbc['qg']], w=[cqn, bq])
            em.op('dve', lambda e: e.scalar_tensor_tensor(out=ckvn[:], in0=bkv.h[:, 0:256], scalar=sst[:, 3:4], in1=kvg[:], op0=ALU.mult, op1=ALU.mult), r=[bkv, sst, kvg], w=[ckvn, bkv])
            em.op('act', lambda e: e.copy(out=kpt[:], in_=bkv.h[:, 256:320]), r=[bkv], w=[kpt])
            em.op('dve', lambda e: e.tensor_scalar(out=angt[:, 0:32], in0=invf32[:], scalar1=posf[:, 0:1], scalar2=None, op0=ALU.mult), r=[invf32, posf], w=[angt])
            em.op('dve', lambda e: e.tensor_scalar(out=angt[:, 32:64], in0=angt[:, 0:32], scalar1=0.5 * PI, scalar2=1.0 / (2 * PI), op0=ALU.add, op1=ALU.mult), r=[angt], w=[angt])
            em.op('dve', lambda e: e.tensor_scalar(out=angt[:, 0:32], in0=angt[:, 0:32], scalar1=1.0 / (2 * PI), scalar2=None, op0=ALU.mult), r=[angt], w=[angt])
            rr_sin(angt, angt[:, :], angti[:, :], angtf[:, :], angti, angtf)
            em.op('dve', lambda e: e.tensor_tensor(out=kpt[:, 0:32], in0=kpt[:, 0:32], in1=angt[:, 32:64], op=ALU.mult), r=[kpt, angt], w=[kpt])
            em.op('dve', lambda e: e.tensor_tensor(out=kpt[:, 32:64], in0=kpt[:, 32:64], in1=angt[:, 0:32], op=ALU.mult), r=[kpt, angt], w=[kpt])
            em.op('dve', lambda e: e.tensor_tensor(out=kpef[:], in0=kpt[:, 0:32], in1=kpt[:, 32:64], op=ALU.add), r=[kpt], w=[kpef])
            em.op('dve', lambda e: e.tensor_copy(out=angf[:], in_=posbi[:]), r=[posbi], w=[angf])
            em.op('dve', lambda e: e.tensor_scalar(out=angf[:], in0=angf[:], scalar1=invf96[0:96, 0:1], scalar2=None, op0=ALU.mult), r=[angf, invf96], w=[angf])
            em.op('dve', lambda e: e.tensor_scalar(out=cst[:, 128:256], in0=angf[:], scalar1=0.5 * PI, scalar2=1.0 / (2 * PI), op0=ALU.add, op1=ALU.mult), r=[angf], w=[cst])
            em.op('dve', lambda e: e.tensor_scalar(out=cst[:, 0:128], in0=angf[:], scalar1=1.0 / (2 * PI), scalar2=None, op0=ALU.mult), r=[angf, cst], w=[cst])
            rr_sin(cst, cst[:, :], csti[:, :], cstf[:, :], csti, cstf)
            if stop <= 8:
                return
            bk = em.bank()
            pt = bk.h[:, :].bitcast(BF16)
            for c in range(4):
                em.op('pe', lambda e, c=c, pt=pt: e.transpose(out=pt[:, c * 128:(c + 1) * 128], in_=cqn[:, c * 128:(c + 1) * 128], identity=ident[:]), r=[cqn, ident], w=[bk], inc=False)
            for c in range(2):
                em.op('pe', lambda e, c=c, pt=pt: e.transpose(out=pt[:, (4 + c) * 128:(5 + c) * 128], in_=ckvn[:, c * 128:(c + 1) * 128], identity=ident[:]), r=[ckvn, ident], w=[bk], inc=False)
            em.op('pe', lambda e, pt=pt: e.transpose(out=pt[0:32, 768:896], in_=kpef[:, 0:32], identity=ident[:]), r=[kpef, ident], w=[bk])
            em.op('dve', lambda e, pt=pt: e.tensor_copy(out=cT[:, 0:6, :].rearrange("p c t -> p (c t)"), in_=pt[:, 0:768]), r=[bk], w=[cT])
            em.op('dve', lambda e, pt=pt: e.tensor_copy(out=cT[0:32, 6, :], in_=pt[0:32, 768:896]), r=[bk, cT], w=[cT])
            cb = cst[:, 128:256].unsqueeze(1).to_broadcast([96, 4, 128])
            sbb = cst[0:32, 0:128].unsqueeze(1).to_broadcast([32, 4, 128])
            qo = qtb
            for g4 in range(2):
                bq1 = em.bank()
                bq2 = em.bank()
                for hh in range(4):
                    h = g4 * 4 + hh
                    for c in range(4):
                        em.op('pe', lambda e, h=h, hh=hh, c=c, bq1=bq1: e.matmul(bq1.h[0:96, hh * 128:(hh + 1) * 128], lhsT=Wq[:, c, h, :], rhs=cT[:, c, :], start=(c == 0), stop=(c == 3)),
                              r=[Wq, cT], w=[bq1], inc=(hh == 3 and c == 3))
                for hh in range(4):
                    h = g4 * 4 + hh
                    for c in range(4):
                        em.op('pe', lambda e, h=h, hh=hh, c=c, bq2=bq2: e.matmul(bq2.h[0:32, hh * 128:(hh + 1) * 128], lhsT=Wqr[:, c, h, :], rhs=cT[:, c, :], start=(c == 0), stop=(c == 3)),
                              r=[Wqr, cT], w=[bq2], inc=(hh == 3 and c == 3))
                hs = slice(g4 * 4, g4 * 4 + 4)
                em.op('dve', lambda e, bq1=bq1, hs=hs: e.tensor_tensor(out=qo[:, hs, :], in0=bq1.h[0:96, :].rearrange("p (a b) -> p a b", a=4), in1=cb, op=ALU.mult), r=[bq1, cst], w=[qo])
                q2v = t_e2[0:32, :].rearrange("p (a b) -> p a b", a=4)
                em.op('dve', lambda e, bq2=bq2, q2v=q2v: e.tensor_tensor(out=q2v, in0=bq2.h[0:32, :].rearrange("p (a b) -> p a b", a=4), in1=sbb, op=ALU.mult), r=[bq2, cst], w=[t_e2])
                em.op('pool', lambda e, hs=hs, q2v=q2v: e.tensor_tensor(out=qo[0:32, hs, :], in0=qo[0:32, hs, :], in1=q2v, op=ALU.add), r=[qo, t_e2], w=[qo])
            em.dma('pool', d_qt[s, :, :, i * 128:(i + 1) * 128], qo[:], r=[qo], w=[db['qt'][s][i]])
            ko = ktb
            for g4 in range(2):
                bk1 = em.bank()
                for hh in range(4):
                    h = g4 * 4 + hh
                    osl = bk1.h[0:96, hh * 128:(hh + 1) * 128]
                    em.op('pe', lambda e, osl=osl: e.matmul(osl, lhsT=selk[:, :], rhs=cT[0:32, 6, :], start=True, stop=False), r=[selk, cT], w=[bk1], inc=False)
                    for c in range(2):
                        em.op('pe', lambda e, osl=osl, c=c, h=h: e.matmul(osl, lhsT=WkN[:, c, h, :], rhs=cT[:, 4 + c, :], start=False, stop=(c == 1)), r=[WkN, cT], w=[bk1],
                              inc=(hh == 3 and c == 1))
                hs = slice(g4 * 4, g4 * 4 + 4)
                em.op('act', lambda e, bk1=bk1, hs=hs: e.copy(out=ko[:, hs, :], in_=bk1.h[0:96, :].rearrange("p (a b) -> p a b", a=4)), r=[bk1], w=[ko])
            em.dma('pool', d_kt[s, :, :, i * 128:(i + 1) * 128], ko[:], r=[ko], w=[db['kt'][s][i]])
            bv = em.bank()
            for c in range(2):
                em.op('pe', lambda e, c=c: e.matmul(bv.h[:, :], lhsT=cT[:, 4 + c, :], rhs=WkV[:, c, :, :].rearrange("p h d -> p (h d)"), start=(c == 0), stop=(c == 1)),
                      r=[cT, WkV], w=[bv], inc=(c == 1))
            vo = vtb
            em.op('act', lambda e: e.copy(out=vo[:, :, 0:64], in_=H8(bv.h[:, :])), r=[bv], w=[vo])
            em.dma('pool', d_v[s, i * 128:(i + 1) * 128, :], vo[:].rearrange("p h d -> p (h d)"), r=[vo], w=[db['v'][s][i]])

        load_x(0, 0)
        for s in range(NSEQ):
            for i in range(NT):
                pass_a_tile(s, i)
        em.flush(pst)

    SC = 96.0 ** -0.5

    def transposes(src_tl, nchunk, dst_tl):
        done = 0
        while done < nchunk:
            m = min(8, nchunk - done)
            bk = em.bank()
            pt = bk.h[:, :].bitcast(BF16)
            for c in range(m):
                em.op('pe', lambda e, c=c, pt=pt, done=done: e.transpose(out=pt[:, c * 128:(c + 1) * 128], in_=src_tl[:, (done + c) * 128:(done + c + 1) * 128], identity=identB[0][:]),
                      r=[src_tl, identB[0]], w=[bk], inc=(c == m - 1))
            em.op('dve', lambda e, pt=pt, done=done, m=m: e.tensor_copy(out=dst_tl[:, done:done + m, :].rearrange("p c t -> p (c t)"), in_=pt[:, 0:m * 128]), r=[bk], w=[dst_tl])
            done += m
    identB = [None]

    def layer_norm(src_tl, dst_ap_tl, gb, bb, st, eps_col, scratch):
        em.op('act', lambda e: e.activation(out=scratch[:], in_=src_tl[:], func=AF.Copy, accum_out=st[:, 0:1]), r=[src_tl], w=[scratch, st])
        em.op('act', lambda e: e.activation(out=scratch[:], in_=src_tl[:], func=AF.Square, accum_out=st[:, 1:2]), r=[src_tl, st], w=[scratch, st])
        em.op('dve', lambda e: e.tensor_scalar(out=st[:, 0:2], in0=st[:, 0:2], scalar1=1.0 / D, scalar2=None, op0=ALU.mult), r=[st], w=[st])
        em.op('dve', lambda e: e.tensor_tensor(out=st[:, 2:3], in0=st[:, 0:1], in1=st[:, 0:1], op=ALU.mult), r=[st], w=[st])
        em.op('dve', lambda e: e.tensor_tensor(out=st[:, 1:2], in0=st[:, 1:2], in1=st[:, 2:3], op=ALU.subtract), r=[st], w=[st])
        em.op('act', lambda e: e.activation(out=st[:, 1:2], in_=st[:, 1:2], func=AF.Sqrt, bias=eps_col, scale=1.0), r=[st], w=[st])
        em.op('dve', lambda e: e.reciprocal(out=st[:, 1:2], in_=st[:, 1:2]), r=[st], w=[st])
        em.op('dve', lambda e: e.tensor_scalar(out=scratch[:], in0=src_tl[:], scalar1=st[:, 0:1], scalar2=st[:, 1:2], op0=ALU.subtract, op1=ALU.mult), r=[src_tl, st], w=[scratch])
        em.op('pool', lambda e: e.tensor_tensor(out=scratch[:], in0=scratch[:], in1=gb[:], op=ALU.mult), r=[scratch, gb], w=[scratch])
        em.op('pool', lambda e: e.tensor_tensor(out=dst_ap_tl[:], in0=scratch[:], in1=bb[:], op=ALU.add), r=[scratch, bb], w=[dst_ap_tl])

    if 'M' in passes:
        with ExitStack() as pst:
            cur_stack[0] = pst
            NQB = S // 512
            KT = sb("KT", [96, NH, S], BF16)
            VV = sb("VV", [128, NT, NH * 65], BF16)
            amask = sb("amask", [128, 4, 512], BF16)
            em.dma('pool', amask[:].rearrange("p a b -> p (a b)"), cin['amask'], w=[amask])
            qTb = [sb("qTb%d" % i, [96, NH, 512], BF16) for i in range(2)]
            pTs = [sb("pTs%d" % i, [128, 512], BF16) for i in range(6)]
            ymb = [sb("ymb%d" % i, [128, 4, 512], BF16) for i in range(2)]
            rcp = sb("rcp", [128, 4])
            npt = 0
            all_banks = list(em.banks)
            em.banks = all_banks[0:5]
            nacc = 0
            LOOK = 2
            for s in range(NSEQ):
                for h in range(NH):
                    em.dma('sp', KT[:, h, :], d_kt[s, :, h, :], r=db['kt'][s], w=[KT])
                em.dma('sp', VV[:], d_v[s].rearrange("(t p) c -> p t c", p=128), r=db['v'][s], w=[VV])
                items = [(qb, h, kt) for qb in range(NQB) for h in range(NH) for kt in range((qb + 1) * 4)]
                loaded = set()
                pts = {}

                def ensure_q(qb):
                    if qb in loaded or qb >= NQB:
                        return
                    loaded.add(qb)
                    qt_ = qTb[qb % 2]
                    em.dma('sp', qt_[:], d_qt[s, :, :, qb * 512:(qb + 1) * 512], r=db['qt'][s][qb * 4:(qb + 1) * 4], w=[qt_])

                def emit_score(idx):
                    nonlocal npt
                    qb, h, kt = items[idx]
                    ensure_q(qb)
                    qt_ = qTb[qb % 2]
                    bs = em.bank()
                    em.op('pe', lambda e, bs=bs, h=h, kt=kt, qt_=qt_: e.matmul(bs.h[:, :], lhsT=KT[:, h, kt * 128:(kt + 1) * 128], rhs=qt_[:, h, :], start=True, stop=True),
                          r=[KT, qt_], w=[bs])
                    pT = pTs[npt % 6]
                    npt += 1
                    em.op('act', lambda e, bs=bs, pT=pT: e.activation(out=pT[:], in_=bs.h[:, :], func=AF.Exp, scale=SC), r=[bs], w=[pT])
                    j = kt - qb * 4
                    if j >= 0:
                        em.op('dve', lambda e, pT=pT, j=j: e.tensor_tensor(out=pT[:], in0=pT[:], in1=amask[:, j, :], op=ALU.mult), r=[pT, amask], w=[pT])
                    pts[idx] = pT

                cur_acc = [None]

                def emit_pv(idx):
                    nonlocal nacc
                    qb, h, kt = items[idx]
                    pT = pts.pop(idx)
                    if kt == 0:
                        cur_acc[0] = all_banks[5 + nacc % 2]
                        nacc += 1
                        if h == 0:
                            ensure_q(qb + 1)
                    bacc = cur_acc[0]
                    first = (kt == 0)
                    for qq in range(4):
                        if kt <= qb * 4 + qq:
                            last = (kt == qb * 4 + qq)
                            em.op('pe', lambda e, bacc=bacc, pT=pT, qq=qq, kt=kt, h=h, first=first, last=last: e.matmul(
                                bacc.h[:, qq * 65:(qq + 1) * 65], lhsT=pT[:, qq * 128:(qq + 1) * 128], rhs=VV[:, kt, h * 65:(h + 1) * 65],
                                start=first, stop=last, skip_group_check=True), r=[pT, VV], w=[bacc], inc=(last or qq == 3))
                            first = False
                    if kt == (qb + 1) * 4 - 1:
                        yo = ymb[qb % 2]
                        accv = bacc.h[:, 0:260].rearrange("p (a b) -> p a b", a=4)
                        em.op('dve', lambda e, accv=accv: e.reciprocal(out=rcp[:], in_=accv[:, :, 64]), r=[bacc], w=[rcp])
                        em.op('dve', lambda e, accv=accv, h=h, yo=yo: e.tensor_tensor(out=yo[:, :, h * 64:(h + 1) * 64], in0=accv[:, :, 0:64],
                                                                                    in1=rcp[:, :].unsqueeze(2).to_broadcast([128, 4, 64]), op=ALU.mult), r=[bacc, rcp], w=[yo])
                        if h == NH - 1:
                            em.dma('pool', d_ym[s, qb * 512:(qb + 1) * 512, :].rearrange("(a p) c -> p a c", p=128), yo[:], r=[yo], w=db['ym'][s][qb * 4:(qb + 1) * 4])

                for idx in range(min(LOOK, len(items))):
                    emit_score(idx)
                for idx in range(len(items)):
                    if idx + LOOK < len(items):
                        emit_score(idx + LOOK)
                    emit_pv(idx)
            em.flush(pst)
            em.banks = all_banks

    if 'B' in passes:
        with ExitStack() as pst:
            cur_stack[0] = pst
            identB[0] = sb("identB", [128, 128], BF16)
            em.dma('pool', identB[0][:], cin['ident'], w=[identB[0]])
            epsB = sb("epsB", [128, 1])
            em.op('dve', lambda e: e.memset(epsB[:], LN_EPS), w=[epsB])
            Wgt = sb("Wgt", [128, 8, 2048], BF16)
            wv = w_in.rearrange("(c p) n -> p c n", p=128)
            for c in range(8):
                em.dma('pool', Wgt[:, c, :], wv[:, c, 2592:4640], w=[Wgt])
            Wpr = sb("Wpr", [128, 4, D], BF16)
            Wpm = sb("Wpm", [128, 4, D], BF16)
            Wo = sb("Wo", [128, 8, D], BF16)
            em.dma('pool', Wpr[:], w_proj_rwkv.rearrange("(c p) n -> p c n", p=128), w=[Wpr])
            em.dma('pool', Wpm[:], w_proj_mla.rearrange("(c p) n -> p c n", p=128), w=[Wpm])
            for c in range(8):
                em.dma('pool', Wo[:, c, :], w_out.rearrange("(c p) n -> p c n", p=128)[:, c, :], w=[Wo])
            g1 = sb("g1", [128, D])
            b1_ = sb("b1_", [128, D])
            bcast_load(g1, ln1_g, D)
            bcast_load(b1_, ln1_b, D)
            def dbl(name, shape, dt=F32):
                return [sb("%s_%d" % (name, k), shape, dt) for k in range(2)]
            xf2 = dbl("xf", [128, D])
            xbb2 = dbl("xbb", [128, D], BF16)
            xTb2 = dbl("xTb", [128, 8, 128], BF16)
            sg2 = dbl("sg", [128, 2048])
            yin2 = dbl("yin", [128, 1024], BF16)
            yT2 = dbl("yT", [128, 8, 128], BF16)
            mg2 = dbl("mg", [128, D])
            mgb2 = dbl("mgb", [128, D], BF16)
            mT2 = dbl("mT", [128, 8, 128], BF16)
            hpre2 = dbl("hpre", [128, D])
            scr2b = dbl("scr", [128, D])
            hout2 = dbl("hout", [128, D])
            stB2 = dbl("stB", [128, 4])
            tilesB = [(s, i) for s in range(NSEQ) for i in range(NT)]

            def b1_s1(t):
                k_ = t % 2
                s, i = tilesB[t]
                xf, xbb, xTb, sg, yin, yT, mg, mgb, mT, hpre, scr, hout, stB = (xf2[k_], xbb2[k_], xTb2[k_], sg2[k_], yin2[k_], yT2[k_], mg2[k_],
                                                                               mgb2[k_], mT2[k_], hpre2[k_], scr2b[k_], hout2[k_], stB2[k_])
                rows = slice(i * 128, (i + 1) * 128)
                em.dma('sp', xf[:], x[s, rows, :], w=[xf])
                em.dma('sp', yin[:, 0:512], d_yrg[s, rows, :], r=[db['yrg'][s][i]], w=[yin])
                em.dma('sp', yin[:, 512:1024], d_ym[s, rows, :], r=[db['ym'][s][i]], w=[yin])
                em.op('act', lambda e, xbb=xbb, xf=xf: e.copy(out=xbb[:], in_=xf[:]), r=[xf], w=[xbb])
                transposes(xbb, 8, xTb)
                for g in range(4):
                    bg = em.bank()
                    for c in range(8):
                        em.op('pe', lambda e, c=c, g=g, bg=bg, xTb=xTb: e.matmul(bg.h[:, :], lhsT=xTb[:, c, :], rhs=Wgt[:, c, g * 512:(g + 1) * 512], start=(c == 0), stop=(c == 7)),
                              r=[xTb, Wgt], w=[bg], inc=(c == 7))
                    em.op('act', lambda e, g=g, bg=bg, sg=sg: e.activation(out=sg[:, g * 512:(g + 1) * 512], in_=bg.h[:, :], func=AF.Sigmoid), r=[bg], w=[sg])
                transposes(yin, 8, yT)
                for which, Wp in ((0, Wpr), (1, Wpm)):
                    for g in range(2):
                        bp = em.bank()
                        for c in range(4):
                            em.op('pe', lambda e, c=c, g=g, bp=bp, Wp=Wp, which=which, yT=yT: e.matmul(bp.h[:, :], lhsT=yT[:, which * 4 + c, :], rhs=Wp[:, c, g * 512:(g + 1) * 512], start=(c == 0), stop=(c == 3)),
                                  r=[yT, Wp], w=[bp], inc=(c == 3))
                        cs = slice(g * 512, (g + 1) * 512)
                        gs = slice(which * 1024 + g * 512, which * 1024 + (g + 1) * 512)
                        if which == 0:
                            em.op('dve', lambda e, bp=bp, cs=cs, gs=gs, mg=mg, sg=sg: e.tensor_tensor(out=mg[:, cs], in0=bp.h[:, :], in1=sg[:, gs], op=ALU.mult), r=[bp, sg], w=[mg])
                        else:
                            em.op('dve', lambda e, bp=bp, cs=cs, gs=gs, scr=scr, sg=sg: e.tensor_tensor(out=scr[:, cs], in0=bp.h[:, :], in1=sg[:, gs], op=ALU.mult), r=[bp, sg], w=[scr])
                em.op('pool', lambda e, mgb=mgb, mg=mg, scr=scr: e.tensor_tensor(out=mgb[:], in0=mg[:], in1=scr[:], op=ALU.add), r=[mg, scr], w=[mgb])

            def b1_s2(t):
                k_ = t % 2
                s, i = tilesB[t]
                xf, xbb, xTb, sg, yin, yT, mg, mgb, mT, hpre, scr, hout, stB = (xf2[k_], xbb2[k_], xTb2[k_], sg2[k_], yin2[k_], yT2[k_], mg2[k_],
                                                                               mgb2[k_], mT2[k_], hpre2[k_], scr2b[k_], hout2[k_], stB2[k_])
                rows = slice(i * 128, (i + 1) * 128)
                transposes(mgb, 8, mT)
                for g in range(2):
                    bo = em.bank()
                    for c in range(8):
                        em.op('pe', lambda e, c=c, g=g, bo=bo, mT=mT: e.matmul(bo.h[:, :], lhsT=mT[:, c, :], rhs=Wo[:, c, g * 512:(g + 1) * 512], start=(c == 0), stop=(c == 7)),
                              r=[mT, Wo], w=[bo], inc=(c == 7))
                    cs = slice(g * 512, (g + 1) * 512)
                    em.op('dve', lambda e, bo=bo, cs=cs, hpre=hpre, xf=xf: e.scalar_tensor_tensor(out=hpre[:, cs], in0=xf[:, cs], scalar=ALPHA, in1=bo.h[:, :], op0=ALU.mult, op1=ALU.add), r=[xf, bo], w=[hpre])
                layer_norm(hpre, hout, g1, b1_, stB, epsB[:, 0:1], scr)
                em.dma('pool', d_h[s, rows, :], hout[:], r=[hout], w=[db['h'][s][i]])

            b1_s1(0)
            for t in range(len(tilesB)):
                if t + 1 < len(tilesB):
                    b1_s1(t + 1)
                b1_s2(t)
            em.flush(pst)

    if 'C' in passes:
        with ExitStack() as pst:
            cur_stack[0] = pst
            identB[0] = sb("identC", [128, 128], BF16)
            em.dma('pool', identB[0][:], cin['ident'], w=[identB[0]])
            epsC = sb("epsC", [128, 1])
            em.op('dve', lambda e: e.memset(epsC[:], LN_EPS), w=[epsC])
            Wfg = sb("Wfg", [128, 8, DFF], BF16)
            Wfu = sb("Wfu", [128, 8, DFF], BF16)
            Wfd = sb("Wfd", [128, 22, D], BF16)
            for c in range(8):
                em.dma('pool', Wfg[:, c, :], w_ffn_gate.rearrange("(c p) n -> p c n", p=128)[:, c, :], w=[Wfg])
                em.dma('pool', Wfu[:, c, :], w_ffn_up.rearrange("(c p) n -> p c n", p=128)[:, c, :], w=[Wfu])
            for c in range(22):
                em.dma('pool', Wfd[:, c, :], w_ffn_down.rearrange("(c p) n -> p c n", p=128)[:, c, :], w=[Wfd])
            g2 = sb("g2", [128, D])
            b2_ = sb("b2_", [128, D])
            bcast_load(g2, ln2_g, D)
            bcast_load(b2_, ln2_b, D)
            def dbl(name, shape, dt=F32):
                return [sb("%s_%d" % (name, k), shape, dt) for k in range(2)]
            hf2 = dbl("hf", [128, D])
            hbb2 = dbl("hbb", [128, D], BF16)
            hT2 = dbl("hT", [128, 8, 128], BF16)
            sil2 = dbl("sil", [128, 512])
            actb2 = dbl("actb", [128, DFF], BF16)
            aT2 = dbl("aT", [128, 22, 128], BF16)
            opre = sb("opre", [128, D])
            scr2 = sb("scr2", [128, D])
            oout2 = dbl("oout", [128, D])
            stC2 = dbl("stC", [128, 4])
            tilesC = [(s, i) for s in range(NSEQ) for i in range(NT)]
            gcn = [0]

            def b2_s1(t):
                k_ = t % 2
                s, i = tilesC[t]
                hf, hbb, hT, actb, aT, oout, stC = hf2[k_], hbb2[k_], hT2[k_], actb2[k_], aT2[k_], oout2[k_], stC2[k_]
                rows = slice(i * 128, (i + 1) * 128)
                em.dma('sp', hf[:], d_h[s, rows, :], r=[db['h'][s][i]], w=[hf])
                em.op('act', lambda e, hbb=hbb, hf=hf: e.copy(out=hbb[:], in_=hf[:]), r=[hf], w=[hbb])
                transposes(hbb, 8, hT)
                for g in range(6):
                    n = 512 if g < 5 else 256
                    c0 = g * 512
                    bg = em.bank()
                    bu = em.bank()
                    sil = sil2[gcn[0] % 2]
                    gcn[0] += 1
                    for c in range(8):
                        em.op('pe', lambda e, c=c, bg=bg, c0=c0, n=n, hT=hT: e.matmul(bg.h[:, 0:n], lhsT=hT[:, c, :], rhs=Wfg[:, c, c0:c0 + n], start=(c == 0), stop=(c == 7)), r=[hT, Wfg], w=[bg], inc=(c == 7))
                    for c in range(8):
                        em.op('pe', lambda e, c=c, bu=bu, c0=c0, n=n, hT=hT: e.matmul(bu.h[:, 0:n], lhsT=hT[:, c, :], rhs=Wfu[:, c, c0:c0 + n], start=(c == 0), stop=(c == 7)), r=[hT, Wfu], w=[bu], inc=(c == 7))
                    em.op('act', lambda e, bg=bg, n=n, sil=sil: e.activation(out=sil[:, 0:n], in_=bg.h[:, 0:n], func=AF.Silu), r=[bg], w=[sil])
                    em.op('dve', lambda e, bu=bu, n=n, c0=c0, sil=sil, actb=actb: e.tensor_tensor(out=actb[:, c0:c0 + n], in0=bu.h[:, 0:n], in1=sil[:, 0:n], op=ALU.mult), r=[bu, sil], w=[actb])

            def b2_s2(t):
                k_ = t % 2
                s, i = tilesC[t]
                hf, hbb, hT, actb, aT, oout, stC = hf2[k_], hbb2[k_], hT2[k_], actb2[k_], aT2[k_], oout2[k_], stC2[k_]
                rows = slice(i * 128, (i + 1) * 128)
                transposes(actb, 22, aT)
                for g in range(2):
                    bo = em.bank()
                    for c in range(22):
                        em.op('pe', lambda e, c=c, g=g, bo=bo, aT=aT: e.matmul(bo.h[:, :], lhsT=aT[:, c, :], rhs=Wfd[:, c, g * 512:(g + 1) * 512], start=(c == 0), stop=(c == 21)),
                              r=[aT, Wfd], w=[bo], inc=(c == 21))
                    cs = slice(g * 512, (g + 1) * 512)
                    em.op('dve', lambda e, bo=bo, cs=cs, hf=hf: e.scalar_tensor_tensor(out=opre[:, cs], in0=hf[:, cs], scalar=ALPHA, in1=bo.h[:, :], op0=ALU.mult, op1=ALU.add), r=[hf, bo], w=[opre])
                layer_norm(opre, oout, g2, b2_, stC, epsC[:, 0:1], scr2)
                em.dma('pool', out[s, rows, :], oout[:], r=[oout], w=[Buf("out")])

            b2_s1(0)
            for t in range(len(tilesC)):
                if t + 1 < len(tilesC):
                    b2_s1(t + 1)
                b2_s2(t)
            em.flush(pst)

    if os.environ.get('CLR1'):
        em.clear_all()
    st.close()
    return nc


_NC_CACHE = {}


def kernel(**inputs):
    S, NSEQ, NCORE = 4096, 2, 8
    if 'nc' not in _NC_CACHE:
        _NC_CACHE['nc'] = build(S, NSEQ)
    nc = _NC_CACHE['nc']
    hc = host_consts()
    base = {}
    for k, v in inputs.items():
        a = np.ascontiguousarray(np.asarray(v))
        if k in ('x', 'positions'):
            continue
        base[k] = a.reshape(a.shape[1:]) if a.shape[0] == 1 else a
    base['r_k'] = base['r_k'].reshape(512)
    for k, v in hc.items():
        base['c_' + k] = v
    xs = np.ascontiguousarray(np.asarray(inputs['x'], dtype=np.float32))
    ps = np.ascontiguousarray(np.asarray(inputs['positions'], dtype=np.int32))
    in_maps = []
    for c in range(NCORE):
        m = dict(base)
        m['x'] = xs[c * NSEQ:(c + 1) * NSEQ]
        m['positions'] = ps[c * NSEQ:(c + 1) * NSEQ]
        in_maps.append(m)
    res = run_bass_kernel_spmd(nc, in_maps, core_ids=list(range(NCORE)))
    return np.concatenate([np.asarray(r['out']) for r in res.results], axis=0).astype(np.float32)
```

```python
import math
import os
from contextlib import ExitStack
import numpy as np
import concourse.bass as bass
import concourse.mybir as mybir
from concourse.bass_utils import run_bass_kernel_spmd

F32 = mybir.dt.float32
BF16 = mybir.dt.bfloat16
I32 = mybir.dt.int32
AF = mybir.ActivationFunctionType
ALU = mybir.AluOpType
AX = mybir.AxisListType

D = 1024
NH = 8
DIN = 4640
DFF = 2816
ALPHA = 2.0 ** 0.25
GN_EPS = 64e-5
LN_EPS = 1e-5
RMS_EPS = 1e-6
PI = math.pi
ENG = ['pe', 'act', 'dve', 'pool', 'sp']


class Buf:
    __slots__ = ('name', 'w', 'r')

    def __init__(self, name):
        self.name = name
        self.w = None
        self.r = {}


class Tl:
    def __init__(self, h, name):
        self.h = h
        self.b = Buf(name)

    def __getitem__(self, k):
        return self.h[k]


class Em:
    def __init__(self, nc, st, ndma=10):
        self.nc = nc
        self.q = {e: [] for e in ENG}
        self.sem = {}
        self.cnt = {}
        for e in ENG:
            self.sem[e] = st.enter_context(nc.semaphore('s_' + e))
            self.cnt[e] = 0
        self.dsem = {}
        for qn in ('sp', 'pool', 'act'):
            self.dsem[qn] = []
            for i in range(ndma):
                n = 'd_%s%d' % (qn, i)
                self.sem[n] = st.enter_context(nc.semaphore(n))
                self.cnt[n] = 0
                self.dsem[qn].append(n)
        self.drr = {qn: 0 for qn in self.dsem}
        self.waited = {e: {} for e in ENG}
        self.nbank = 0
        self.banks = []

    def _deps(self, eng, reads, writes, is_dma=False):
        need = {}

        def add(s, v):
            if need.get(s, 0) < v:
                need[s] = v
        for b in reads:
            if b.w is not None:
                add(*b.w)
        for b in writes:
            if b.w is not None:
                add(*b.w)
            for s, v in b.r.items():
                if s == eng and not is_dma:
                    continue
                add(s, v)
        if eng == 'pe' and not is_dma:
            need.pop('pe', None)
        waits = []
        wd = self.waited[eng]
        for s, v in need.items():
            if wd.get(s, 0) >= v:
                continue
            wd[s] = v
            waits.append((s, v))
        return waits

    def _mark(self, tok, reads, writes):
        s, v = tok
        for b in reads:
            if b.r.get(s, 0) < v:
                b.r[s] = v
        for b in writes:
            b.w = tok
            b.r = {}

    def op(self, eng, fn, r=(), w=(), inc=True):
        r = [x.b if isinstance(x, Tl) else x for x in r]
        w = [x.b if isinstance(x, Tl) else x for x in w]
        waits = self._deps(eng, r, w)
        if inc:
            self.cnt[eng] += 1
            tok = (eng, self.cnt[eng])
        else:
            tok = (eng, self.cnt[eng] + 1)
        self.q[eng].append((waits, fn, (eng, 1) if inc else None))
        self._mark(tok, r, w)

    def dma(self, qn, out, in_, r=(), w=()):
        r = [x.b if isinstance(x, Tl) else x for x in r]
        w = [x.b if isinstance(x, Tl) else x for x in w]
        s = self.dsem[qn][self.drr[qn]]
        self.drr[qn] = (self.drr[qn] + 1) % len(self.dsem[qn])
        waits = self._deps(qn, r, w, is_dma=True)
        wd = self.waited[qn]
        if self.cnt[s] > 0 and wd.get(s, 0) < self.cnt[s]:
            wd[s] = self.cnt[s]
            waits.append((s, self.cnt[s]))
        self.cnt[s] += 16
        tok = (s, self.cnt[s])
        self.q[qn].append((waits, (lambda e, o=out, i=in_: e.dma_start(out=o, in_=i)), (s, 16)))
        self._mark(tok, r, w)

    def barrier(self):
        for e in ENG:
            waits = []
            for s, c in self.cnt.items():
                if s == e or c == 0:
                    continue
                if self.waited[e].get(s, 0) < c:
                    self.waited[e][s] = c
                    waits.append((s, c))
            if waits:
                self.q[e].append((waits, None, None))

    def flush(self, st):
        self.barrier()
        block = st.enter_context(self.nc.Block())

        def mk(eng):
            items = self.q[eng]
            sem = self.sem

            def run(e):
                for waits, fn, inc in items:
                    for s, v in waits:
                        e.wait_ge(sem[s], v)
                    if fn is None:
                        continue
                    ins = fn(e)
                    if inc is not None:
                        ins.then_inc(sem[inc[0]], inc[1])
            return run
        block.tensor(mk('pe'))
        block.scalar(mk('act'))
        block.vector(mk('dve'))
        block.gpsimd(mk('pool'))
        block.sync(mk('sp'))
        self.q = {e: [] for e in ENG}

    def clear_all(self):
        self.nc.all_engine_barrier()
        for s in self.sem.values():
            self.nc.gpsimd.sem_clear(s)
        self.nc.all_engine_barrier()

    def bank(self):
        b = self.banks[self.nbank % len(self.banks)]
        self.nbank += 1
        return b


def host_consts():
    c = {}
    s = np.arange(128)[:, None]
    t = np.arange(128)[None, :]
    c['ident'] = np.eye(128, dtype=np.float32)
    msu = (s < t).astype(np.float32)
    miu = (s <= t).astype(np.float32)
    msl = (t < s).astype(np.float32)
    c['mask3'] = np.concatenate([msu, miu, msl], axis=1)
    c['tri'] = miu.copy()
    invf = (10000.0 ** (-np.arange(0, 32, 2, dtype=np.float32) / 32.0)).astype(np.float32)
    c['invf32'] = np.tile(np.concatenate([invf, invf])[None, :], (128, 1)).astype(np.float32)
    i96 = np.zeros((128, 1), np.float32)
    i96[0:16, 0] = invf
    i96[16:32, 0] = invf
    c['invf96'] = i96
    sel = np.zeros((32, 96), np.float32)
    sel[np.arange(32), np.arange(32)] = 1.0
    c['selk'] = sel
    am = np.zeros((128, 4, 512), np.float32)
    for j in range(4):
        am[:, j, :] = ((j * 128 + np.arange(128))[:, None] <= np.arange(512)[None, :])
    c['amask'] = am.reshape(128, 2048)
    return c


def build(S, NSEQ, dbg=False, stop=99, passes='AMBC'):
    NT = S // 128
    nc = bass.Bass("TRN2", target_bir_lowering=False)
    st = ExitStack()
    em = Em(nc, st)
    import os
    if os.environ.get('CLR0'):
        em.clear_all()

    def din(name, shape, dt=F32):
        return nc.dram_tensor(name, list(shape), dt, kind="ExternalInput").ap()

    x = din("x", [NSEQ, S, D])
    positions = din("positions", [NSEQ, S], I32)
    w_in = din("w_in", [D, DIN])
    mu_shift = din("mu_shift", [1792])
    w_decay_up = din("w_decay_up", [64, 512])
    w_decay_base = din("w_decay_base", [512])
    w_aaa_up = din("w_aaa_up", [64, 512])
    w_aaa_base = din("w_aaa_base", [512])
    w_gate_up = din("w_gate_up", [128, 512])
    k_k = din("k_k", [512])
    k_a = din("k_a", [512])
    r_k = din("r_k", [512])
    lnx_g = din("lnx_g", [512])
    lnx_b = din("lnx_b", [512])
    q_norm_g = din("q_norm_g", [512])
    w_uq = din("w_uq", [512, 768])
    kv_norm_g = din("kv_norm_g", [256])
    w_ukv = din("w_ukv", [256, 1024])
    if 'B' in passes:
        w_proj_rwkv = din("w_proj_rwkv", [512, D])
        w_proj_mla = din("w_proj_mla", [512, D])
        w_out = din("w_out", [D, D])
        ln1_g = din("ln1_g", [D])
        ln1_b = din("ln1_b", [D])
    if 'C' in passes:
        w_ffn_gate = din("w_ffn_gate", [D, DFF])
        w_ffn_up = din("w_ffn_up", [D, DFF])
        w_ffn_down = din("w_ffn_down", [DFF, D])
        ln2_g = din("ln2_g", [D])
        ln2_b = din("ln2_b", [D])
    hc = host_consts()
    cin = {k: din("c_" + k, v.shape) for k, v in hc.items()}
    out = nc.dram_tensor("out", [NSEQ, S, D], F32, kind="ExternalOutput").ap()

    okind = "ExternalOutput" if dbg else "Internal"

    def dscr(name, shape, dt):
        return nc.dram_tensor(name, list(shape), dt, kind=okind).ap()
    d_yrg = dscr("d_yrg", [NSEQ, S, 512], BF16)
    d_qt = dscr("d_qt", [NSEQ, 96, NH, S], BF16)
    d_kt = dscr("d_kt", [NSEQ, 96, NH, S], BF16)
    d_v = dscr("d_v", [NSEQ, S, NH * 65], BF16)
    d_gate = dscr("d_gate", [NSEQ, S, 2048], F32)
    d_ym = dscr("d_ym", [NSEQ, S, 512], BF16)
    d_h = dscr("d_h", [NSEQ, S, D], F32)
    db = {n: [[Buf("%s_%d_%d" % (n, s, i)) for i in range(NT)] for s in range(NSEQ)]
          for n in ('yrg', 'qt', 'kt', 'v', 'gate', 'ym', 'h')}

    for i in range(7):
        em.banks.append(Tl(nc.alloc_psum_tensor("bank%d" % i, [128, 512], F32), "bank%d" % i))
    fbank = Tl(nc.alloc_psum_tensor("fbank", [128, 512], F32), "fbank")
    fcol = [None]

    def fence():
        o = fcol[0]
        em.op('pe', lambda e: e.matmul(fbank.h[0:1, 0:1], lhsT=o[:, 0:1], rhs=o[:, 0:1], start=True, stop=True), r=[o], w=[fbank], inc=True)

    cur_stack = [None]

    def sb(name, shape, dt=F32):
        return Tl(cur_stack[0].enter_context(nc.sbuf_tensor(name, list(shape), dt)), name)

    def bcast_load(dst, src1d, n, q='sp'):
        em.dma(q, dst[:, 0:n], src1d.partition_broadcast(128), w=[dst])

    with ExitStack() as pst:
        cur_stack[0] = pst
        ident = sb("ident", [128, 128], BF16)
        mask3 = sb("mask3", [128, 384])
        tri = sb("tri", [128, 128], F32)
        ones1 = sb("ones1", [128, 1], F32)
        invf32 = sb("invf32", [128, 32], F32)
        invf96 = sb("invf96", [128, 1], F32)
        selk = sb("selk", [32, 96], BF16)
        zrow = sb("zrow", [1, 1792], F32)
        em.dma('pool', ident[:], cin['ident'], w=[ident])
        em.dma('sp', mask3[:], cin['mask3'], w=[mask3])
        em.dma('sp', tri[:], cin['tri'], w=[tri])
        em.dma('sp', invf32[:], cin['invf32'], w=[invf32])
        em.dma('sp', invf96[:], cin['invf96'], w=[invf96])
        em.dma('pool', selk[:], cin['selk'], w=[selk])
        em.op('dve', lambda e: e.memset(ones1[:], 1.0), w=[ones1])
        fcol[0] = ones1
        epsc = sb("epsc", [128, 4])
        em.op('dve', lambda e: e.memset(epsc[:, 0:1], GN_EPS), w=[epsc])
        em.op('dve', lambda e: e.memset(epsc[:, 1:2], RMS_EPS), r=[epsc], w=[epsc])
        em.op('dve', lambda e: e.memset(epsc[:, 2:3], LN_EPS), r=[epsc], w=[epsc])
        em.op('dve', lambda e: e.memset(epsc[:, 3:4], -PI), r=[epsc], w=[epsc])
        em.op('dve', lambda e: e.memset(zrow[:], 0.0), w=[zrow])

        Win = sb("Win", [128, 8, 2624], BF16)
        wv = w_in.rearrange("(c p) n -> p c n", p=128)
        for c in range(8):
            em.dma('pool', Win[:, c, 0:2592], wv[:, c, 0:2592], w=[Win])
        em.op('act', lambda e: e.mul(out=Win[:, :, 2592:2608], in_=Win[:, :, 2576:2592], mul=-1.0), r=[Win], w=[Win])
        em.op('act', lambda e: e.copy(out=Win[:, :, 2608:2624], in_=Win[:, :, 2560:2576]), r=[Win], w=[Win])

        Wq = sb("Wq", [128, 4, NH, 96], BF16)
        Wqr = sb("Wqr", [128, 4, NH, 32], BF16)
        qv = w_uq.rearrange("(c p) (h d) -> p c h d", p=128, h=NH)
        for c in range(4):
            em.dma('pool', Wq[:, c, :, 0:32], qv[:, c, :, 64:96], w=[Wq])
            em.dma('pool', Wq[:, c, :, 32:96], qv[:, c, :, 0:64], w=[Wq])
        em.op('act', lambda e: e.mul(out=Wqr[:, :, :, 0:16], in_=Wq[:, :, :, 16:32], mul=-1.0), r=[Wq, Wqr], w=[Wqr])
        em.op('act', lambda e: e.copy(out=Wqr[:, :, :, 16:32], in_=Wq[:, :, :, 0:16]), r=[Wq, Wqr], w=[Wqr])
        WkN = sb("WkN", [128, 2, NH, 96], BF16)
        WkV = sb("WkV", [128, 2, NH, 64], BF16)
        kvv = w_ukv.rearrange("(c p) (h e) -> p c h e", p=128, h=NH)
        em.op('dve', lambda e: e.memset(WkN[:], 0.0), w=[WkN])
        for c in range(2):
            em.dma('pool', WkN[:, c, :, 32:96], kvv[:, c, :, 0:64], r=[WkN], w=[WkN])
            em.dma('pool', WkV[:, c, :, :], kvv[:, c, :, 64:128], w=[WkV])
        Wlo = sb("Wlo", [128, 512], BF16)
        Wg = sb("Wg", [128, 512], BF16)
        em.dma('pool', Wlo[0:64, :], w_decay_up, w=[Wlo])
        em.dma('pool', Wlo[64:128, :], w_aaa_up, w=[Wlo])
        em.dma('pool', Wg[:], w_gate_up, w=[Wg])

        mu_bc = sb("mu_bc", [128, 1792])
        bcast_load(mu_bc, mu_shift, 1792)
        omu_bc = sb("omu_bc", [128, 1792])
        em.op('dve', lambda e: e.tensor_scalar(out=omu_bc[:], in0=mu_bc[:], scalar1=-1.0, scalar2=1.0, op0=ALU.mult, op1=ALU.add), r=[mu_bc], w=[omu_bc])
        zcm = sb("zcm", [128, 1792])
        pbc = {}
        for nm, src in (('kk', k_k), ('ka', k_a), ('rk', r_k), ('lg', lnx_g), ('lb', lnx_b),
                        ('bw', w_decay_base), ('ba', w_aaa_base), ('qg', q_norm_g)):
            pbc[nm] = sb("bc_" + nm, [128, 512])
            bcast_load(pbc[nm], src, 512)
        kvg = sb("bc_kvg", [128, 256])
        bcast_load(kvg, kv_norm_g, 256)

        xb = [sb("xb%d" % i, [128, 1024], BF16) for i in range(2)]
        xT = sb("xT", [128, 8, 128], BF16)
        zc = sb("zc", [128, 1792])
        zlast = sb("zlast", [1, 1792])
        zs = sb("zs", [128, 1792])
        loin = sb("loin", [128, 256], BF16)
        loT = sb("loT", [128, 2, 128], BF16)
        t_a = sb("t_a", [128, 512])
        t_lw = sb("t_lw", [128, 512])
        t_g = sb("t_g", [128, 512])
        t_kk = sb("t_kk", [128, 512])
        t_k = sb("t_k", [128, 512])
        t_tmp = sb("t_tmp", [128, 512])
        t_tmp2 = sb("t_tmp2", [128, 512])
        t_e1 = sb("t_e1", [128, 512])
        t_e2 = sb("t_e2", [128, 512])
        t_e3 = sb("t_e3", [128, 512])
        st8 = sb("st8", [128, 8, 8])
        tmA = sb("tmA", [128, 512], BF16)
        tmR = sb("tmR", [128, 512], BF16)
        tmB = sb("tmB", [128, 512], BF16)
        tmK = sb("tmK", [128, 512], BF16)
        Vb = sb("Vb", [128, 512], BF16)
        AT = sb("AT", [128, 4, 512], BF16)
        gC = sb("gC", [128, 4])
        M12 = sb("M12", [128, NH, 512], BF16)
        Xs = sb("Xs", [128, NH, 128], BF16)
        XTs = sb("XTs", [128, NH, 128], BF16)
        TT = sb("TT", [128, NH, 128], BF16)
        Hf = sb("Hf", [128, 4, 64])
        Hb = sb("Hb", [128, 4, 64], BF16)
        Htmp = sb("Htmp", [128, 4, 64])
        RHSb = sb("RHSb", [128, NH, 64], BF16)
        Ub = sb("Ub", [128, NH, 64], BF16)
        yrg = [sb("yrg%d" % i, [128, 512], BF16) for i in range(2)]
        posi = sb("posi", [128, 1], I32)
        posf = sb("posf", [128, 1])
        posbi = sb("posbi", [96, 128], I32)
        angt = sb("angt", [128, 64])
        cst = sb("cst", [96, 256])
        csti = sb("csti", [96, 256], I32)
        cstf = sb("cstf", [96, 256])
        angti = sb("angti", [128, 64], I32)
        angtf = sb("angtf", [128, 64])
        angf = sb("angf", [96, 128])
        cqn = sb("cqn", [128, 512], BF16)
        ckvn = sb("ckvn", [128, 256], BF16)
        kpef = sb("kpef", [128, 32], BF16)
        kpt = sb("kpt", [128, 64])
        cT = sb("cT", [128, 7, 128], BF16)
        sst = sb("sst", [128, 4])
        qtb = sb("qtb", [96, NH, 128], BF16)
        ktb = sb("ktb", [96, NH, 128], BF16)
        vtb = sb("vtb", [128, NH, 65], BF16)
        em.op('dve', lambda e: e.memset(vtb[:], 1.0), w=[vtb])

        H8 = lambda ap: ap.rearrange("p (h d) -> p h d", h=NH)
        dumped = set()

        def dump(name, tl, shape, dt=F32, view=None):
            import os
            if not dbg or name in dumped or os.environ.get('NODUMP'):
                return
            dumped.add(name)
            dd = nc.dram_tensor("g_" + name, list(shape), dt, kind="ExternalOutput").ap()
            em.dma('sp', dd, (tl[:] if view is None else view), r=[tl], w=[Buf("g_" + name)])

        def rr_sin(T, y, yi, yf, Ti, Tf):
            em.op('dve', lambda e: e.tensor_copy(out=yi, in_=y), r=[T], w=[Ti])
            em.op('dve', lambda e: e.tensor_copy(out=yf, in_=yi), r=[Ti], w=[Tf])
            em.op('dve', lambda e: e.tensor_tensor(out=y, in0=y, in1=yf, op=ALU.subtract), r=[T, Tf], w=[T])
            em.op('dve', lambda e: e.tensor_scalar(out=yf, in0=y, scalar1=0.5, scalar2=-1.0, op0=ALU.is_gt, op1=ALU.mult), r=[T], w=[Tf])
            em.op('dve', lambda e: e.tensor_tensor(out=y, in0=y, in1=yf, op=ALU.add), r=[T, Tf], w=[T])
            em.op('dve', lambda e: e.tensor_scalar(out=yf, in0=y, scalar1=-0.5, scalar2=None, op0=ALU.is_lt), r=[T], w=[Tf])
            em.op('dve', lambda e: e.tensor_tensor(out=y, in0=y, in1=yf, op=ALU.add), r=[T, Tf], w=[T])
            em.op('dve', lambda e: e.tensor_scalar(out=y, in0=y, scalar1=-0.5, scalar2=0.5, op0=ALU.max, op1=ALU.min), r=[T], w=[T])
            em.op('act', lambda e: e.activation(out=y, in_=y, func=AF.Sin, scale=2 * PI), r=[T], w=[T])

        def load_x(s, i):
            em.dma('pool', xb[i % 2][:], x[s, i * 128:(i + 1) * 128, :], w=[xb[i % 2]])

        gn = [0]

        def pass_a_tile(s, i):
            par = i % 2
            xbt = xb[par]
            bk = em.bank()
            pt = bk.h[:, :].bitcast(BF16)
            for c in range(8):
                em.op('pe', lambda e, pt=pt, c=c: e.transpose(out=pt[:, c * 128:(c + 1) * 128], in_=xbt[:, c * 128:(c + 1) * 128], identity=ident[:]),
                      r=[xbt, ident], w=[bk], inc=(c == 7))
            em.op('act', lambda e, pt=pt: e.copy(out=xT[:].rearrange("p c t -> p (c t)"), in_=pt), r=[bk], w=[xT])
            dump("xT", xT, [128, 8, 128], BF16)
            dump("xb", xbt, [128, 1024], BF16)
            if i + 1 < NT:
                load_x(s, i + 1)
            elif s + 1 < NSEQ:
                load_x(s + 1, 0)

            def proj(c0, n):
                b = em.bank()
                for c in range(8):
                    em.op('pe', lambda e, c=c, b=b: e.matmul(b.h[:, 0:n], lhsT=xT[:, c, :], rhs=Win[:, c, c0:c0 + n], start=(c == 0), stop=(c == 7)),
                          r=[xT, Win], w=[b], inc=(c == 7))
                return b
            zct = zc
            for g in range(4):
                n = 512 if g < 3 else 256
                b = proj(g * 512, n)
                em.op('act', lambda e, b=b, g=g, n=n: e.copy(out=zct[:, g * 512:g * 512 + n], in_=b.h[:, 0:n]), r=[b], w=[zct])
            dump("zc", zct, [128, 1792])
            dump("Win0", Win, [128, 512], BF16, view=Win[:, 0, 0:512])
            dump("Win7", Win, [128, 512], BF16, view=Win[:, 7, 2112:2624])
            em.dma('sp', zs[1:128, :], zct[0:127, :], r=[zct], w=[zs])
            if i == 0:
                em.dma('sp', zs[0:1, :], zrow[:], r=[zrow], w=[zs])
            else:
                em.dma('sp', zs[0:1, :], zlast[:], r=[zlast], w=[zs])
            em.dma('sp', zlast[:], zct[127:128, :], r=[zct], w=[zlast])
            em.op('pool', lambda e: e.tensor_tensor(out=zcm[:], in0=zct[:], in1=omu_bc[:], op=ALU.mult), r=[zct, omu_bc], w=[zcm])
            em.op('dve', lambda e: e.tensor_tensor(out=zs[:], in0=zs[:], in1=mu_bc[:], op=ALU.mult), r=[zs, mu_bc], w=[zs])
            em.op('dve', lambda e: e.tensor_tensor(out=zs[:], in0=zs[:], in1=zcm[:], op=ALU.add), r=[zs, zcm], w=[zs])
            zr = zs[:, 0:512]
            zk = zs[:, 512:1024]
            zv = zs[:, 1024:1536]
            if stop <= 1:
                return
            em.op('act', lambda e: e.activation(out=loin[:, 0:64], in_=zs[:, 1536:1600], func=AF.Tanh), r=[zs], w=[loin])
            em.op('act', lambda e: e.activation(out=loin[:, 128:256], in_=zs[:, 1664:1792], func=AF.Sigmoid), r=[zs], w=[loin])
            em.op('act', lambda e: e.copy(out=loin[:, 64:128], in_=zs[:, 1600:1664]), r=[zs], w=[loin])
            if stop <= 1.2:
                return
            bk = em.bank()
            pt = bk.h[:, :].bitcast(BF16)
            for c in range(2):
                em.op('pe', lambda e, pt=pt, c=c: e.transpose(out=pt[:, c * 128:(c + 1) * 128], in_=loin[:, c * 128:(c + 1) * 128], identity=ident[:]),
                      r=[loin, ident], w=[bk], inc=(c == 1))
            em.op('dve', lambda e, pt=pt: e.tensor_copy(out=loT[:].rearrange("p c t -> p (c t)"), in_=pt[:, 0:256]), r=[bk], w=[loT])
            if stop <= 1.4:
                return
            b_w = em.bank()
            em.op('pe', lambda e: e.matmul(b_w.h[:, :], lhsT=loT[0:64, 0, :], rhs=Wlo[0:64, :], start=True, stop=True), r=[loT, Wlo], w=[b_w], inc=False)
            b_a = em.bank()
            em.op('pe', lambda e: e.matmul(b_a.h[:, :], lhsT=loT[64:128, 0, :], rhs=Wlo[64:128, :], start=True, stop=True), r=[loT, Wlo], w=[b_a], inc=False)
            b_g = em.bank()
            em.op('pe', lambda e: e.matmul(b_g.h[:, :], lhsT=loT[:, 1, :], rhs=Wg[:, :], start=True, stop=True), r=[loT, Wg], w=[b_g])
            if stop <= 1.6:
                return
            em.op('dve', lambda e: e.tensor_tensor(out=t_lw[:], in0=b_w.h[:, :], in1=pbc['bw'][:], op=ALU.add), r=[b_w, pbc['bw']], w=[t_lw])
            if stop <= 1.7:
                return
            em.op('act', lambda e: e.activation(out=t_lw[:], in_=t_lw[:], func=AF.Sigmoid), r=[t_lw], w=[t_lw])
            if stop <= 1.8:
                return
            em.op('dve', lambda e: e.tensor_scalar(out=t_lw[:], in0=t_lw[:], scalar1=-math.exp(-0.5), scalar2=None, op0=ALU.mult), r=[t_lw], w=[t_lw])
            em.op('dve', lambda e: e.tensor_tensor(out=t_a[:], in0=b_a.h[:, :], in1=pbc['ba'][:], op=ALU.add), r=[b_a, pbc['ba']], w=[t_a])
            em.op('act', lambda e: e.activation(out=t_a[:], in_=t_a[:], func=AF.Sigmoid), r=[t_a], w=[t_a])
            if stop <= 1.9:
                return
            em.op('act', lambda e: e.copy(out=t_g[:], in_=b_g.h[:, :]), r=[b_g], w=[t_g])
            if stop <= 2:
                return
            em.op('pool', lambda e: e.tensor_tensor(out=t_kk[:], in0=zk, in1=pbc['kk'][:], op=ALU.mult), r=[zs, pbc['kk']], w=[t_kk])
            em.op('pool', lambda e: e.tensor_tensor(out=t_tmp[:], in0=t_kk[:], in1=t_kk[:], op=ALU.mult), r=[t_kk], w=[t_tmp])
            em.op('dve', lambda e: e.tensor_reduce(out=st8[:, 0, :], in_=H8(t_tmp[:]), axis=AX.X, op=ALU.add), r=[t_tmp], w=[st8])
            em.op('act', lambda e: e.activation(out=st8[:, 1, :], in_=st8[:, 0, :], func=AF.Sqrt), r=[st8], w=[st8])
            em.op('dve', lambda e: e.tensor_scalar(out=st8[:, 1, :], in0=st8[:, 1, :], scalar1=1e-12, scalar2=None, op0=ALU.max), r=[st8], w=[st8])
            em.op('dve', lambda e: e.reciprocal(out=st8[:, 2, :], in_=st8[:, 1, :]), r=[st8], w=[st8])
            em.op('dve', lambda e: e.tensor_tensor(out=H8(t_kk[:]), in0=H8(t_kk[:]), in1=st8[:, 2, :].unsqueeze(2).to_broadcast([128, NH, 64]), op=ALU.mult),
                  r=[t_kk, st8], w=[t_kk])
            em.op('dve', lambda e: e.scalar_tensor_tensor(out=t_k[:], in0=t_a[:], scalar=-1.0, in1=pbc['ka'][:], op0=ALU.add, op1=ALU.mult), r=[t_a, pbc['ka']], w=[t_k])
            em.op('dve', lambda e: e.scalar_tensor_tensor(out=t_k[:], in0=t_k[:], scalar=1.0, in1=zk, op0=ALU.add, op1=ALU.mult), r=[t_k, zs], w=[t_k])
            em.op('pool', lambda e: e.tensor_tensor(out=t_tmp[:], in0=zr, in1=t_k[:], op=ALU.mult), r=[zs, t_k, st8], w=[t_tmp])
            em.op('pool', lambda e: e.tensor_tensor(out=t_tmp[:], in0=t_tmp[:], in1=pbc['rk'][:], op=ALU.mult), r=[t_tmp, pbc['rk']], w=[t_tmp])
            em.op('dve', lambda e: e.tensor_reduce(out=st8[:, 3, :], in_=H8(t_tmp[:]), axis=AX.X, op=ALU.add), r=[t_tmp], w=[st8])
            b_c = em.bank()
            em.op('pe', lambda e: e.matmul(b_c.h[:, :], lhsT=tri[:, :], rhs=t_lw[:, :], start=True, stop=True), r=[tri, t_lw], w=[b_c])
            b_gc = em.bank()
            for p in range(4):
                em.op('pe', lambda e, p=p: e.matmul(b_gc.h[:, p:p + 1], lhsT=t_lw[:, p * 128:(p + 1) * 128], rhs=ones1[:, :], start=True, stop=True),
                      r=[t_lw, ones1], w=[b_gc], inc=(p == 3))
            em.op('act', lambda e: e.activation(out=gC[:], in_=b_gc.h[:, 0:4], func=AF.Exp), r=[b_gc], w=[gC])
            em.op('act', lambda e: e.activation(out=t_e1[:], in_=b_c.h[:, :], func=AF.Exp), r=[b_c], w=[t_e1])
            em.op('act', lambda e: e.activation(out=t_e2[:], in_=b_c.h[:, :], func=AF.Exp, scale=-1.0), r=[b_c], w=[t_e2])
            em.op('dve', lambda e: e.tensor_tensor(out=t_e3[:], in0=b_c.h[:, :], in1=t_lw[:], op=ALU.subtract), r=[b_c, t_lw], w=[t_e3, b_c])
            em.op('act', lambda e: e.activation(out=t_e3[:], in_=t_e3[:], func=AF.Exp), r=[t_e3], w=[t_e3])
            em.op('pool', lambda e: e.tensor_tensor(out=tmR[:], in0=zr, in1=t_e1[:], op=ALU.mult), r=[zs, t_e1], w=[tmR])
            em.op('pool', lambda e: e.tensor_tensor(out=tmK[:], in0=t_k[:], in1=t_e2[:], op=ALU.mult), r=[t_k, t_e2], w=[tmK])
            em.op('dve', lambda e: e.tensor_tensor(out=t_tmp2[:], in0=t_kk[:], in1=t_a[:], op=ALU.mult), r=[t_kk, t_a], w=[t_tmp2])
            em.op('dve', lambda e: e.tensor_tensor(out=tmB[:], in0=t_tmp2[:], in1=t_e2[:], op=ALU.mult), r=[t_tmp2, t_e2], w=[tmB])
            em.op('dve', lambda e: e.scalar_tensor_tensor(out=tmA[:], in0=t_kk[:], scalar=-1.0, in1=t_e3[:], op0=ALU.mult, op1=ALU.mult), r=[t_kk, t_e3], w=[tmA])
            em.op('act', lambda e: e.copy(out=Vb[:], in_=zv), r=[zs], w=[Vb])
            if stop <= 3:
                return
            dump("zs", zs, [128, 1792])
            dump("lw", t_lw, [128, 512])
            dump("a", t_a, [128, 512])
            dump("kk", t_kk, [128, 512])
            dump("k", t_k, [128, 512])
            dump("e1", t_e1, [128, 512])
            dump("e2", t_e2, [128, 512])
            dump("e3", t_e3, [128, 512])
            dump("tmA", tmA, [128, 512], BF16)
            dump("tmB", tmB, [128, 512], BF16)
            dump("gC", gC, [128, 4])
            for half in range(2):
                bk = em.bank()
                pt = bk.h[:, :].bitcast(BF16)
                n = 0
                for pp in range(2):
                    p = half * 2 + pp
                    for j, src in enumerate((tmA, tmR, tmB, tmK)):
                        n += 1
                        em.op('pe', lambda e, pt=pt, p=p, pp=pp, j=j, src=src: e.transpose(out=pt[:, pp * 512 + j * 128: pp * 512 + (j + 1) * 128],
                                                                                  in_=src[:, p * 128:(p + 1) * 128], identity=ident[:]),
                              r=[src, ident], w=[bk], inc=(n == 8))
                eng = 'dve'
                if eng == 'act':
                    em.op('act', lambda e, half=half, pt=pt: e.copy(out=AT[:, half * 2:half * 2 + 2, :].rearrange("p a b -> p (a b)"), in_=pt), r=[bk], w=[AT])
                else:
                    em.op('dve', lambda e, half=half, pt=pt: e.tensor_copy(out=AT[:, half * 2:half * 2 + 2, :].rearrange("p a b -> p (a b)"), in_=pt), r=[bk], w=[AT])
            if stop <= 3.2:
                return
            msk2 = mask3[:, 0:256].unsqueeze(1).to_broadcast([128, 2, 256])
            for hp in range(4):
                bb = []
                for q in range(2):
                    h = 2 * hp + q
                    ps = slice(q * 64, (q + 1) * 64)
                    b1 = em.bank()
                    bb.append((h, b1))
                    em.op('pe', lambda e, ps=ps, b1=b1, hp=hp: e.matmul(b1.h[:, 0:256], lhsT=AT[ps, hp, 256:384], rhs=AT[ps, hp, 0:256], start=True, stop=True),
                          r=[AT], w=[b1], inc=False)
                    em.op('pe', lambda e, ps=ps, b1=b1, hp=hp: e.matmul(b1.h[:, 256:512], lhsT=AT[ps, hp, 384:512], rhs=AT[ps, hp, 0:256], start=True, stop=True),
                          r=[AT], w=[b1], inc=False)
                fence()
                for h, b1 in bb:
                    em.op('dve', lambda e, b1=b1, h=h: e.tensor_tensor(out=M12[:, h, :].rearrange("p (a b) -> p a b", a=2), in0=b1.h[:, :].rearrange("p (a b) -> p a b", a=2), in1=msk2, op=ALU.mult),
                          r=[b1, mask3], w=[M12])
            if stop <= 3.4:
                return
            msl4 = mask3[:, 256:384].unsqueeze(1).to_broadcast([128, 4, 128])
            X0, XT0 = Xs, XTs
            for q in range(2):
                b3 = em.bank()
                ps = slice(q * 64, (q + 1) * 64)
                for hh in range(4):
                    h = 2 * hh + q
                    em.op('pe', lambda e, h=h, hh=hh, ps=ps, b3=b3: e.matmul(b3.h[:, hh * 128:(hh + 1) * 128], lhsT=AT[ps, h // 2, 0:128], rhs=AT[ps, h // 2, 256:384], start=True, stop=True),
                          r=[AT], w=[b3], inc=False)
                fence()
                em.op('dve', lambda e, b3=b3, q=q: e.tensor_tensor(out=X0[:, q:NH:2, :], in0=b3.h[:, :].rearrange("p (a b) -> p a b", a=4), in1=msl4, op=ALU.mult),
                      r=[b3, mask3], w=[X0])
            if stop <= 3.6:
                return
            em.op('pool', lambda e: e.tensor_copy(out=XT0[:], in_=M12[:, :, 0:128]), r=[M12], w=[XT0])
            em.op('pool', lambda e: e.tensor_tensor(out=TT[:], in0=M12[:, :, 0:128], in1=ident[:].unsqueeze(1).to_broadcast([128, NH, 128]), op=ALU.add),
                  r=[M12, ident], w=[TT])
            if stop <= 4:
                return
            dump("AT", AT, [128, 4, 512], BF16)
            dump("M12", M12, [128, NH, 512], BF16)
            dump("X0", Xs, [128, NH, 128], BF16)
            dump("TT0", TT, [128, NH, 128], BF16)
            for lvl in range(1, 7):
                bxs, bxts, bts = [], [], []
                for g4 in range(2):
                    bx = em.bank()
                    bxs.append(bx)
                    for hh in range(4):
                        h = g4 * 4 + hh
                        em.op('pe', lambda e, h=h, hh=hh, bx=bx: e.matmul(bx.h[:, hh * 128:(hh + 1) * 128], lhsT=XTs[:, h, :], rhs=Xs[:, h, :], start=True, stop=True),
                              r=[Xs, XTs], w=[bx], inc=(hh == 3))
                    if lvl < 6:
                        bxt = em.bank()
                        bxts.append(bxt)
                        for hh in range(4):
                            h = g4 * 4 + hh
                            em.op('pe', lambda e, h=h, hh=hh, bxt=bxt: e.matmul(bxt.h[:, hh * 128:(hh + 1) * 128], lhsT=Xs[:, h, :], rhs=XTs[:, h, :], start=True, stop=True),
                                  r=[Xs, XTs], w=[bxt], inc=(hh == 3))
                for g4 in range(2):
                    hs = slice(g4 * 4, g4 * 4 + 4)
                    em.op('act', lambda e, bx=bxs[g4], hs=hs: e.copy(out=Xs[:, hs, :], in_=bx.h[:, :].rearrange("p (a b) -> p a b", a=4)), r=[bxs[g4]], w=[Xs])
                    if lvl < 6:
                        em.op('dve', lambda e, bxt=bxts[g4], hs=hs: e.tensor_copy(out=XTs[:, hs, :], in_=bxt.h[:, :].rearrange("p (a b) -> p a b", a=4)), r=[bxts[g4]], w=[XTs])
                for g4 in range(2):
                    bt = em.bank()
                    bts.append(bt)
                    for hh in range(4):
                        h = g4 * 4 + hh
                        em.op('pe', lambda e, h=h, hh=hh, bt=bt: e.matmul(bt.h[:, hh * 128:(hh + 1) * 128], lhsT=Xs[:, h, :], rhs=TT[:, h, :], start=True, stop=True),
                              r=[Xs, TT], w=[bt], inc=(hh == 3))
                for g4 in range(2):
                    hs = slice(g4 * 4, g4 * 4 + 4)
                    em.op('dve', lambda e, bt=bts[g4], hs=hs: e.tensor_tensor(out=TT[:, hs, :], in0=bt.h[:, :].rearrange("p (a b) -> p a b", a=4), in1=TT[:, hs, :], op=ALU.add),
                          r=[bts[g4], TT], w=[TT])
            dump("TT", TT, [128, NH, 128], BF16)
            if i == 0:
                em.op('dve', lambda e: e.memset(Hf[:], 0.0), w=[Hf])
                em.op('dve', lambda e: e.memset(Hb[:], 0.0), w=[Hb])
            bR = em.bank()
            for h in range(NH):
                ps = slice((h % 2) * 64, (h % 2) * 64 + 64)
                em.op('pe', lambda e, h=h, ps=ps: e.matmul(bR.h[:, h * 64:(h + 1) * 64], lhsT=AT[ps, h // 2, 0:128], rhs=Hb[ps, h // 2, :], start=True, stop=False),
                      r=[AT, Hb], w=[bR], inc=False)
                em.op('pe', lambda e, h=h: e.matmul(bR.h[:, h * 64:(h + 1) * 64], lhsT=M12[:, h, 256:384], rhs=Vb[:, h * 64:(h + 1) * 64], start=False, stop=True),
                      r=[M12, Vb], w=[bR], inc=(h == NH - 1))
            em.op('act', lambda e: e.copy(out=RHSb[:].rearrange("p h d -> p (h d)"), in_=bR.h[:, :]), r=[bR], w=[RHSb])
            bU = em.bank()
            for h in range(NH):
                em.op('pe', lambda e, h=h: e.matmul(bU.h[:, h * 64:(h + 1) * 64], lhsT=TT[:, h, :], rhs=RHSb[:, h, :], start=True, stop=True),
                      r=[TT, RHSb], w=[bU], inc=(h == NH - 1))
            em.op('act', lambda e: e.copy(out=Ub[:].rearrange("p h d -> p (h d)"), in_=bU.h[:, :]), r=[bU], w=[Ub])
            bY = em.bank()
            for h in range(NH):
                ps = slice((h % 2) * 64, (h % 2) * 64 + 64)
                hsl = slice(h * 64, (h + 1) * 64)
                em.op('pe', lambda e, h=h, ps=ps, hsl=hsl: e.matmul(bY.h[:, hsl], lhsT=AT[ps, h // 2, 128:256], rhs=Hb[ps, h // 2, :], start=True, stop=False),
                      r=[AT, Hb], w=[bY], inc=False)
                em.op('pe', lambda e, h=h, hsl=hsl: e.matmul(bY.h[:, hsl], lhsT=M12[:, h, 128:256], rhs=Ub[:, h, :], start=False, stop=False),
                      r=[M12, Ub], w=[bY], inc=False)
                em.op('pe', lambda e, h=h, hsl=hsl: e.matmul(bY.h[:, hsl], lhsT=M12[:, h, 384:512], rhs=Vb[:, hsl], start=False, stop=True),
                      r=[M12, Vb], w=[bY], inc=(h == NH - 1))
            bH = em.bank()
            for p in range(4):
                psl = slice(p * 128, (p + 1) * 128)
                em.op('pe', lambda e, p=p, psl=psl: e.matmul(bH.h[:, psl], lhsT=tmB[:, psl], rhs=Ub[:, 2 * p:2 * p + 2, :].rearrange("p a b -> p (a b)"), start=True, stop=False),
                      r=[tmB, Ub], w=[bH], inc=False)
                em.op('pe', lambda e, p=p, psl=psl: e.matmul(bH.h[:, psl], lhsT=tmK[:, psl], rhs=Vb[:, psl], start=False, stop=True),
                      r=[tmK, Vb], w=[bH], inc=(p == 3))
            bHv = bH.h[:, :].rearrange("p (a b) -> p a b", a=4)
            em.op('dve', lambda e: e.tensor_tensor(out=Htmp[0:64, :, :], in0=bHv[0:64, :, 0:64], in1=Hf[0:64, :, :], op=ALU.add), r=[bH, Hf], w=[Htmp])
            em.op('dve', lambda e: e.tensor_tensor(out=Htmp[64:128, :, :], in0=bHv[64:128, :, 64:128], in1=Hf[64:128, :, :], op=ALU.add), r=[bH, Hf], w=[Htmp])
            gcb = gC[:, :].unsqueeze(2).to_broadcast([128, 4, 64])
            em.op('dve', lambda e: e.tensor_tensor(out=Hf[:], in0=Htmp[:], in1=gcb, op=ALU.mult), r=[Htmp, gC], w=[Hf])
            em.op('pool', lambda e: e.tensor_copy(out=Hb[:], in_=Hf[:]), r=[Hf], w=[Hb])
            if stop <= 6:
                return
            dump("RHSb", RHSb, [128, NH, 64], BF16)
            dump("Ub", Ub, [128, NH, 64], BF16)
            dump("Hf", Hf, [128, 4, 64])
            yv = H8(bY.h[:, :])
            em.op('act', lambda e: e.copy(out=t_e1[:], in_=bY.h[:, :]), r=[bY], w=[t_e1])
            dump('y', t_e1, [128, 512])
            em.op('dve', lambda e: e.tensor_reduce(out=st8[:, 4, :], in_=H8(t_e1[:]), axis=AX.X, op=ALU.add), r=[t_e1], w=[st8])
            em.op('pool', lambda e: e.tensor_tensor(out=t_e2[:], in0=t_e1[:], in1=t_e1[:], op=ALU.mult), r=[t_e1], w=[t_e2])
            em.op('dve', lambda e: e.tensor_reduce(out=st8[:, 5, :], in_=H8(t_e2[:]), axis=AX.X, op=ALU.add), r=[t_e2], w=[st8])
            em.op('dve', lambda e: e.tensor_scalar(out=st8[:, 4, :], in0=st8[:, 4, :], scalar1=1.0 / 64, scalar2=None, op0=ALU.mult), r=[st8], w=[st8])
            em.op('dve', lambda e: e.tensor_tensor(out=st8[:, 6, :], in0=st8[:, 4, :], in1=st8[:, 4, :], op=ALU.mult), r=[st8], w=[st8])
            em.op('dve', lambda e: e.scalar_tensor_tensor(out=st8[:, 5, :], in0=st8[:, 5, :], scalar=1.0 / 64, in1=st8[:, 6, :], op0=ALU.mult, op1=ALU.subtract), r=[st8], w=[st8])
            em.op('act', lambda e: e.activation(out=st8[:, 5, :], in_=st8[:, 5, :], func=AF.Sqrt, bias=epsc[:, 0:1], scale=1.0), r=[st8, epsc], w=[st8])
            em.op('dve', lambda e: e.reciprocal(out=st8[:, 5, :], in_=st8[:, 5, :]), r=[st8], w=[st8])
            em.op('dve', lambda e: e.tensor_tensor(out=H8(t_e1[:]), in0=H8(t_e1[:]), in1=st8[:, 4, :].unsqueeze(2).to_broadcast([128, NH, 64]), op=ALU.subtract), r=[t_e1, st8], w=[t_e1])
            em.op('dve', lambda e: e.tensor_tensor(out=H8(t_e1[:]), in0=H8(t_e1[:]), in1=st8[:, 5, :].unsqueeze(2).to_broadcast([128, NH, 64]), op=ALU.mult), r=[t_e1, st8], w=[t_e1])
            em.op('pool', lambda e: e.tensor_tensor(out=t_e1[:], in0=t_e1[:], in1=pbc['lg'][:], op=ALU.mult), r=[t_e1, pbc['lg']], w=[t_e1])
            em.op('pool', lambda e: e.tensor_tensor(out=t_e1[:], in0=t_e1[:], in1=pbc['lb'][:], op=ALU.add), r=[t_e1, pbc['lb']], w=[t_e1])
            em.op('dve', lambda e: e.tensor_tensor(out=H8(t_e2[:]), in0=H8(zv), in1=st8[:, 3, :].unsqueeze(2).to_broadcast([128, NH, 64]), op=ALU.mult), r=[zs, st8], w=[t_e2])
            em.op('pool', lambda e: e.tensor_tensor(out=t_e1[:], in0=t_e1[:], in1=t_e2[:], op=ALU.add), r=[t_e1, t_e2], w=[t_e1])
            yo = yrg[par]
            em.op('pool', lambda e: e.tensor_tensor(out=yo[:], in0=t_e1[:], in1=t_g[:], op=ALU.mult), r=[t_e1, t_g], w=[yo])
            em.dma('pool', d_yrg[s, i * 128:(i + 1) * 128, :], yo[:], r=[yo], w=[db['yrg'][s][i]])

            if stop <= 7:
                return
            em.dma('sp', posi[:], positions[s, i * 128:(i + 1) * 128].unsqueeze(1), w=[posi])
            em.dma('sp', posbi[:], positions[s, i * 128:(i + 1) * 128].partition_broadcast(96), w=[posbi])
            em.op('dve', lambda e: e.tensor_copy(out=posf[:], in_=posi[:]), r=[posi], w=[posf])
            bq = proj(1792, 512)
            bkv = proj(2304, 320)
            em.op('act', lambda e: e.activation(out=t_tmp[:], in_=bq.h[:, :], func=AF.Square, accum_out=sst[:, 0:1]), r=[bq], w=[t_tmp, sst])
            em.op('act', lambda e: e.activation(out=t_tmp[:, 0:256], in_=bkv.h[:, 0:256], func=AF.Square, accum_out=sst[:, 1:2]), r=[bkv, sst], w=[t_tmp, sst])
            em.op('act', lambda e: e.activation(out=sst[:, 2:3], in_=sst[:, 0:1], func=AF.Sqrt, bias=epsc[:, 1:2], scale=1.0 / 512), r=[sst, epsc], w=[sst])
            em.op('act', lambda e: e.activation(out=sst[:, 3:4], in_=sst[:, 1:2], func=AF.Sqrt, bias=epsc[:, 1:2], scale=1.0 / 256), r=[sst, epsc], w=[sst])
            em.op('dve', lambda e: e.reciprocal(out=sst[:, 2:4], in_=sst[:, 2:4]), r=[sst], w=[sst])
            em.op('dve', lambda e: e.scalar_tensor_tensor(out=cqn[:], in0=bq.h[:, :], scalar=sst[:, 2:3], in1=pbc['qg'][:], op0=ALU.mult, op1=ALU.mult), r=[bq, sst, pbc['qg']], w=[cqn, bq])
            em.op('dve', lambda e: e.scalar_tensor_tensor(out=ckvn[:], in0=bkv.h[:, 0:256], scalar=sst[:, 3:4], in1=kvg[:], op0=ALU.mult, op1=ALU.mult), r=[bkv, sst, kvg], w=[ckvn, bkv])
            em.op('act', lambda e: e.copy(out=kpt[:], in_=bkv.h[:, 256:320]), r=[bkv], w=[kpt])
            em.op('dve', lambda e: e.tensor_scalar(out=angt[:, 0:32], in0=invf32[:], scalar1=posf[:, 0:1], scalar2=None, op0=ALU.mult), r=[invf32, posf], w=[angt])
            em.op('dve', lambda e: e.tensor_scalar(out=angt[:, 32:64], in0=angt[:, 0:32], scalar1=0.5 * PI, scalar2=1.0 / (2 * PI), op0=ALU.add, op1=ALU.mult), r=[angt], w=[angt])
            em.op('dve', lambda e: e.tensor_scalar(out=angt[:, 0:32], in0=angt[:, 0:32], scalar1=1.0 / (2 * PI), scalar2=None, op0=ALU.mult), r=[angt], w=[angt])
            rr_sin(angt, angt[:, :], angti[:, :], angtf[:, :], angti, angtf)
            em.op('dve', lambda e: e.tensor_tensor(out=kpt[:, 0:32], in0=kpt[:, 0:32], in1=angt[:, 32:64], op=ALU.mult), r=[kpt, angt], w=[kpt])
            em.op('dve', lambda e: e.tensor_tensor(out=kpt[:, 32:64], in0=kpt[:, 32:64], in1=angt[:, 0:32], op=ALU.mult), r=[kpt, angt], w=[kpt])
            em.op('dve', lambda e: e.tensor_tensor(out=kpef[:], in0=kpt[:, 0:32], in1=kpt[:, 32:64], op=ALU.add), r=[kpt], w=[kpef])
            em.op('dve', lambda e: e.tensor_copy(out=angf[:], in_=posbi[:]), r=[posbi], w=[angf])
            em.op('dve', lambda e: e.tensor_scalar(out=angf[:], in0=angf[:], scalar1=invf96[0:96, 0:1], scalar2=None, op0=ALU.mult), r=[angf, invf96], w=[angf])
            em.op('dve', lambda e: e.tensor_scalar(out=cst[:, 128:256], in0=angf[:], scalar1=0.5 * PI, scalar2=1.0 / (2 * PI), op0=ALU.add, op1=ALU.mult), r=[angf], w=[cst])
            em.op('dve', lambda e: e.tensor_scalar(out=cst[:, 0:128], in0=angf[:], scalar1=1.0 / (2 * PI), scalar2=None, op0=ALU.mult), r=[angf, cst], w=[cst])
            rr_sin(cst, cst[:, :], csti[:, :], cstf[:, :], csti, cstf)
            if stop <= 8:
                return
            bk = em.bank()
            pt = bk.h[:, :].bitcast(BF16)
            for c in range(4):
                em.op('pe', lambda e, c=c, pt=pt: e.transpose(out=pt[:, c * 128:(c + 1) * 128], in_=cqn[:, c * 128:(c + 1) * 128], identity=ident[:]), r=[cqn, ident], w=[bk], inc=False)
            for c in range(2):
                em.op('pe', lambda e, c=c, pt=pt: e.transpose(out=pt[:, (4 + c) * 128:(5 + c) * 128], in_=ckvn[:, c * 128:(c + 1) * 128], identity=ident[:]), r=[ckvn, ident], w=[bk], inc=False)
            em.op('pe', lambda e, pt=pt: e.transpose(out=pt[0:32, 768:896], in_=kpef[:, 0:32], identity=ident[:]), r=[kpef, ident], w=[bk])
            em.op('dve', lambda e, pt=pt: e.tensor_copy(out=cT[:, 0:6, :].rearrange("p c t -> p (c t)"), in_=pt[:, 0:768]), r=[bk], w=[cT])
            em.op('dve', lambda e, pt=pt: e.tensor_copy(out=cT[0:32, 6, :], in_=pt[0:32, 768:896]), r=[bk, cT], w=[cT])
            cb = cst[:, 128:256].unsqueeze(1).to_broadcast([96, 4, 128])
            sbb = cst[0:32, 0:128].unsqueeze(1).to_broadcast([32, 4, 128])
            qo = qtb
            for g4 in range(2):
                bq1 = em.bank()
                bq2 = em.bank()
                for hh in range(4):
                    h = g4 * 4 + hh
                    for c in range(4):
                        em.op('pe', lambda e, h=h, hh=hh, c=c, bq1=bq1: e.matmul(bq1.h[0:96, hh * 128:(hh + 1) * 128], lhsT=Wq[:, c, h, :], rhs=cT[:, c, :], start=(c == 0), stop=(c == 3)),
                              r=[Wq, cT], w=[bq1], inc=(hh == 3 and c == 3))
                for hh in range(4):
                    h = g4 * 4 + hh
                    for c in range(4):
                        em.op('pe', lambda e, h=h, hh=hh, c=c, bq2=bq2: e.matmul(bq2.h[0:32, hh * 128:(hh + 1) * 128], lhsT=Wqr[:, c, h, :], rhs=cT[:, c, :], start=(c == 0), stop=(c == 3)),
                              r=[Wqr, cT], w=[bq2], inc=(hh == 3 and c == 3))
                hs = slice(g4 * 4, g4 * 4 + 4)
                em.op('dve', lambda e, bq1=bq1, hs=hs: e.tensor_tensor(out=qo[:, hs, :], in0=bq1.h[0:96, :].rearrange("p (a b) -> p a b", a=4), in1=cb, op=ALU.mult), r=[bq1, cst], w=[qo])
                q2v = t_e2[0:32, :].rearrange("p (a b) -> p a b", a=4)
                em.op('dve', lambda e, bq2=bq2, q2v=q2v: e.tensor_tensor(out=q2v, in0=bq2.h[0:32, :].rearrange("p (a b) -> p a b", a=4), in1=sbb, op=ALU.mult), r=[bq2, cst], w=[t_e2])
                em.op('pool', lambda e, hs=hs, q2v=q2v: e.tensor_tensor(out=qo[0:32, hs, :], in0=qo[0:32, hs, :], in1=q2v, op=ALU.add), r=[qo, t_e2], w=[qo])
            em.dma('pool', d_qt[s, :, :, i * 128:(i + 1) * 128], qo[:], r=[qo], w=[db['qt'][s][i]])
            ko = ktb
            for g4 in range(2):
                bk1 = em.bank()
                for hh in range(4):
                    h = g4 * 4 + hh
                    osl = bk1.h[0:96, hh * 128:(hh + 1) * 128]
                    em.op('pe', lambda e, osl=osl: e.matmul(osl, lhsT=selk[:, :], rhs=cT[0:32, 6, :], start=True, stop=False), r=[selk, cT], w=[bk1], inc=False)
                    for c in range(2):
                        em.op('pe', lambda e, osl=osl, c=c, h=h: e.matmul(osl, lhsT=WkN[:, c, h, :], rhs=cT[:, 4 + c, :], start=False, stop=(c == 1)), r=[WkN, cT], w=[bk1],
                              inc=(hh == 3 and c == 1))
                hs = slice(g4 * 4, g4 * 4 + 4)
                em.op('act', lambda e, bk1=bk1, hs=hs: e.copy(out=ko[:, hs, :], in_=bk1.h[0:96, :].rearrange("p (a b) -> p a b", a=4)), r=[bk1], w=[ko])
            em.dma('pool', d_kt[s, :, :, i * 128:(i + 1) * 128], ko[:], r=[ko], w=[db['kt'][s][i]])
            bv = em.bank()
            for c in range(2):
                em.op('pe', lambda e, c=c: e.matmul(bv.h[:, :], lhsT=cT[:, 4 + c, :], rhs=WkV[:, c, :, :].rearrange("p h d -> p (h d)"), start=(c == 0), stop=(c == 1)),
                      r=[cT, WkV], w=[bv], inc=(c == 1))
            vo = vtb
            em.op('act', lambda e: e.copy(out=vo[:, :, 0:64], in_=H8(bv.h[:, :])), r=[bv], w=[vo])
            em.dma('pool', d_v[s, i * 128:(i + 1) * 128, :], vo[:].rearrange("p h d -> p (h d)"), r=[vo], w=[db['v'][s][i]])

        load_x(0, 0)
        for s in range(NSEQ):
            for i in range(NT):
                pass_a_tile(s, i)
        em.flush(pst)

    SC = 96.0 ** -0.5

    def transposes(src_tl, nchunk, dst_tl):
        done = 0
        while done < nchunk:
            m = min(8, nchunk - done)
            bk = em.bank()
            pt = bk.h[:, :].bitcast(BF16)
            for c in range(m):
                em.op('pe', lambda e, c=c, pt=pt, done=done: e.transpose(out=pt[:, c * 128:(c + 1) * 128], in_=src_tl[:, (done + c) * 128:(done + c + 1) * 128], identity=identB[0][:]),
                      r=[src_tl, identB[0]], w=[bk], inc=(c == m - 1))
            em.op('dve', lambda e, pt=pt, done=done, m=m: e.tensor_copy(out=dst_tl[:, done:done + m, :].rearrange("p c t -> p (c t)"), in_=pt[:, 0:m * 128]), r=[bk], w=[dst_tl])
            done += m
    identB = [None]

    def layer_norm(src_tl, dst_ap_tl, gb, bb, st, eps_col, scratch):
        em.op('act', lambda e: e.activation(out=scratch[:], in_=src_tl[:], func=AF.Copy, accum_out=st[:, 0:1]), r=[src_tl], w=[scratch, st])
        em.op('act', lambda e: e.activation(out=scratch[:], in_=src_tl[:], func=AF.Square, accum_out=st[:, 1:2]), r=[src_tl, st], w=[scratch, st])
        em.op('dve', lambda e: e.tensor_scalar(out=st[:, 0:2], in0=st[:, 0:2], scalar1=1.0 / D, scalar2=None, op0=ALU.mult), r=[st], w=[st])
        em.op('dve', lambda e: e.tensor_tensor(out=st[:, 2:3], in0=st[:, 0:1], in1=st[:, 0:1], op=ALU.mult), r=[st], w=[st])
        em.op('dve', lambda e: e.tensor_tensor(out=st[:, 1:2], in0=st[:, 1:2], in1=st[:, 2:3], op=ALU.subtract), r=[st], w=[st])
        em.op('act', lambda e: e.activation(out=st[:, 1:2], in_=st[:, 1:2], func=AF.Sqrt, bias=eps_col, scale=1.0), r=[st], w=[st])
        em.op('dve', lambda e: e.reciprocal(out=st[:, 1:2], in_=st[:, 1:2]), r=[st], w=[st])
        em.op('dve', lambda e: e.tensor_scalar(out=scratch[:], in0=src_tl[:], scalar1=st[:, 0:1], scalar2=st[:, 1:2], op0=ALU.subtract, op1=ALU.mult), r=[src_tl, st], w=[scratch])
        em.op('pool', lambda e: e.tensor_tensor(out=scratch[:], in0=scratch[:], in1=gb[:], op=ALU.mult), r=[scratch, gb], w=[scratch])
        em.op('pool', lambda e: e.tensor_tensor(out=dst_ap_tl[:], in0=scratch[:], in1=bb[:], op=ALU.add), r=[scratch, bb], w=[dst_ap_tl])

    if 'M' in passes:
        with ExitStack() as pst:
            cur_stack[0] = pst
            NQB = S // 512
            KT = sb("KT", [96, NH, S], BF16)
            VV = sb("VV", [128, NT, NH * 65], BF16)
            amask = sb("amask", [128, 4, 512], BF16)
            em.dma('pool', amask[:].rearrange("p a b -> p (a b)"), cin['amask'], w=[amask])
            qTb = [sb("qTb%d" % i, [96, NH, 512], BF16) for i in range(2)]
            pTs = [sb("pTs%d" % i, [128, 512], BF16) for i in range(6)]
            ymb = [sb("ymb%d" % i, [128, 4, 512], BF16) for i in range(2)]
            rcp = sb("rcp", [128, 4])
            npt = 0
            all_banks = list(em.banks)
            em.banks = all_banks[0:5]
            nacc = 0
            LOOK = 2
            for s in range(NSEQ):
                for h in range(NH):
                    em.dma('sp', KT[:, h, :], d_kt[s, :, h, :], r=db['kt'][s], w=[KT])
                em.dma('sp', VV[:], d_v[s].rearrange("(t p) c -> p t c", p=128), r=db['v'][s], w=[VV])
                items = [(qb, h, kt) for qb in range(NQB) for h in range(NH) for kt in range((qb + 1) * 4)]
                loaded = set()
                pts = {}

                def ensure_q(qb):
                    if qb in loaded or qb >= NQB:
                        return
                    loaded.add(qb)
                    qt_ = qTb[qb % 2]
                    em.dma('sp', qt_[:], d_qt[s, :, :, qb * 512:(qb + 1) * 512], r=db['qt'][s][qb * 4:(qb + 1) * 4], w=[qt_])

                def emit_score(idx):
                    nonlocal npt
                    qb, h, kt = items[idx]
                    ensure_q(qb)
                    qt_ = qTb[qb % 2]
                    bs = em.bank()
                    em.op('pe', lambda e, bs=bs, h=h, kt=kt, qt_=qt_: e.matmul(bs.h[:, :], lhsT=KT[:, h, kt * 128:(kt + 1) * 128], rhs=qt_[:, h, :], start=True, stop=True),
                          r=[KT, qt_], w=[bs])
                    pT = pTs[npt % 6]
                    npt += 1
                    em.op('act', lambda e, bs=bs, pT=pT: e.activation(out=pT[:], in_=bs.h[:, :], func=AF.Exp, scale=SC), r=[bs], w=[pT])
                    j = kt - qb * 4
                    if j >= 0:
                        em.op('dve', lambda e, pT=pT, j=j: e.tensor_tensor(out=pT[:], in0=pT[:], in1=amask[:, j, :], op=ALU.mult), r=[pT, amask], w=[pT])
                    pts[idx] = pT

                cur_acc = [None]

                def emit_pv(idx):
                    nonlocal nacc
                    qb, h, kt = items[idx]
                    pT = pts.pop(idx)
                    if kt == 0:
                        cur_acc[0] = all_banks[5 + nacc % 2]
                        nacc += 1
                        if h == 0:
                            ensure_q(qb + 1)
                    bacc = cur_acc[0]
                    first = (kt == 0)
                    for qq in range(4):
                        if kt <= qb * 4 + qq:
                            last = (kt == qb * 4 + qq)
                            em.op('pe', lambda e, bacc=bacc, pT=pT, qq=qq, kt=kt, h=h, first=first, last=last: e.matmul(
                                bacc.h[:, qq * 65:(qq + 1) * 65], lhsT=pT[:, qq * 128:(qq + 1) * 128], rhs=VV[:, kt, h * 65:(h + 1) * 65],
                                start=first, stop=last, skip_group_check=True), r=[pT, VV], w=[bacc], inc=(last or qq == 3))
                            first = False
                    if kt == (qb + 1) * 4 - 1:
                        yo = ymb[qb % 2]
                        accv = bacc.h[:, 0:260].rearrange("p (a b) -> p a b", a=4)
                        em.op('dve', lambda e, accv=accv: e.reciprocal(out=rcp[:], in_=accv[:, :, 64]), r=[bacc], w=[rcp])
                        em.op('dve', lambda e, accv=accv, h=h, yo=yo: e.tensor_tensor(out=yo[:, :, h * 64:(h + 1) * 64], in0=accv[:, :, 0:64],
                                                                                    in1=rcp[:, :].unsqueeze(2).to_broadcast([128, 4, 64]), op=ALU.mult), r=[bacc, rcp], w=[yo])
                        if h == NH - 1:
                            em.dma('pool', d_ym[s, qb * 512:(qb + 1) * 512, :].rearrange("(a p) c -> p a c", p=128), yo[:], r=[yo], w=db['ym'][s][qb * 4:(qb + 1) * 4])

                for idx in range(min(LOOK, len(items))):
                    emit_score(idx)
                for idx in range(len(items)):
                    if idx + LOOK < len(items):
                        emit_score(idx + LOOK)
                    emit_pv(idx)
            em.flush(pst)
            em.banks = all_banks

    if 'B' in passes:
        with ExitStack() as pst:
            cur_stack[0] = pst
            identB[0] = sb("identB", [128, 128], BF16)
            em.dma('pool', identB[0][:], cin['ident'], w=[identB[0]])
            epsB = sb("epsB", [128, 1])
            em.op('dve', lambda e: e.memset(epsB[:], LN_EPS), w=[epsB])
            Wgt = sb("Wgt", [128, 8, 2048], BF16)
            wv = w_in.rearrange("(c p) n -> p c n", p=128)
            for c in range(8):
                em.dma('pool', Wgt[:, c, :], wv[:, c, 2592:4640], w=[Wgt])
            Wpr = sb("Wpr", [128, 4, D], BF16)
            Wpm = sb("Wpm", [128, 4, D], BF16)
            Wo = sb("Wo", [128, 8, D], BF16)
            em.dma('pool', Wpr[:], w_proj_rwkv.rearrange("(c p) n -> p c n", p=128), w=[Wpr])
            em.dma('pool', Wpm[:], w_proj_mla.rearrange("(c p) n -> p c n", p=128), w=[Wpm])
            for c in range(8):
                em.dma('pool', Wo[:, c, :], w_out.rearrange("(c p) n -> p c n", p=128)[:, c, :], w=[Wo])
            g1 = sb("g1", [128, D])
            b1_ = sb("b1_", [128, D])
            bcast_load(g1, ln1_g, D)
            bcast_load(b1_, ln1_b, D)
            def dbl(name, shape, dt=F32):
                return [sb("%s_%d" % (name, k), shape, dt) for k in range(2)]
            xf2 = dbl("xf", [128, D])
            xbb2 = dbl("xbb", [128, D], BF16)
            xTb2 = dbl("xTb", [128, 8, 128], BF16)
            sg2 = dbl("sg", [128, 2048])
            yin2 = dbl("yin", [128, 1024], BF16)
            yT2 = dbl("yT", [128, 8, 128], BF16)
            mg2 = dbl("mg", [128, D])
            mgb2 = dbl("mgb", [128, D], BF16)
            mT2 = dbl("mT", [128, 8, 128], BF16)
            hpre2 = dbl("hpre", [128, D])
            scr2b = dbl("scr", [128, D])
            hout2 = dbl("hout", [128, D])
            stB2 = dbl("stB", [128, 4])
            tilesB = [(s, i) for s in range(NSEQ) for i in range(NT)]

            def b1_s1(t):
                k_ = t % 2
                s, i = tilesB[t]
                xf, xbb, xTb, sg, yin, yT, mg, mgb, mT, hpre, scr, hout, stB = (xf2[k_], xbb2[k_], xTb2[k_], sg2[k_], yin2[k_], yT2[k_], mg2[k_],
                                                                               mgb2[k_], mT2[k_], hpre2[k_], scr2b[k_], hout2[k_], stB2[k_])
                rows = slice(i * 128, (i + 1) * 128)
                em.dma('sp', xf[:], x[s, rows, :], w=[xf])
                em.dma('sp', yin[:, 0:512], d_yrg[s, rows, :], r=[db['yrg'][s][i]], w=[yin])
                em.dma('sp', yin[:, 512:1024], d_ym[s, rows, :], r=[db['ym'][s][i]], w=[yin])
                em.op('act', lambda e, xbb=xbb, xf=xf: e.copy(out=xbb[:], in_=xf[:]), r=[xf], w=[xbb])
                transposes(xbb, 8, xTb)
                for g in range(4):
                    bg = em.bank()
                    for c in range(8):
                        em.op('pe', lambda e, c=c, g=g, bg=bg, xTb=xTb: e.matmul(bg.h[:, :], lhsT=xTb[:, c, :], rhs=Wgt[:, c, g * 512:(g + 1) * 512], start=(c == 0), stop=(c == 7)),
                              r=[xTb, Wgt], w=[bg], inc=(c == 7))
                    em.op('act', lambda e, g=g, bg=bg, sg=sg: e.activation(out=sg[:, g * 512:(g + 1) * 512], in_=bg.h[:, :], func=AF.Sigmoid), r=[bg], w=[sg])
                transposes(yin, 8, yT)
                for which, Wp in ((0, Wpr), (1, Wpm)):
                    for g in range(2):
                        bp = em.bank()
                        for c in range(4):
                            em.op('pe', lambda e, c=c, g=g, bp=bp, Wp=Wp, which=which, yT=yT: e.matmul(bp.h[:, :], lhsT=yT[:, which * 4 + c, :], rhs=Wp[:, c, g * 512:(g + 1) * 512], start=(c == 0), stop=(c == 3)),
                                  r=[yT, Wp], w=[bp], inc=(c == 3))
                        cs = slice(g * 512, (g + 1) * 512)
                        gs = slice(which * 1024 + g * 512, which * 1024 + (g + 1) * 512)
                        if which == 0:
                            em.op('dve', lambda e, bp=bp, cs=cs, gs=gs, mg=mg, sg=sg: e.tensor_tensor(out=mg[:, cs], in0=bp.h[:, :], in1=sg[:, gs], op=ALU.mult), r=[bp, sg], w=[mg])
                        else:
                            em.op('dve', lambda e, bp=bp, cs=cs, gs=gs, scr=scr, sg=sg: e.tensor_tensor(out=scr[:, cs], in0=bp.h[:, :], in1=sg[:, gs], op=ALU.mult), r=[bp, sg], w=[scr])
                em.op('pool', lambda e, mgb=mgb, mg=mg, scr=scr: e.tensor_tensor(out=mgb[:], in0=mg[:], in1=scr[:], op=ALU.add), r=[mg, scr], w=[mgb])

            def b1_s2(t):
                k_ = t % 2
                s, i = tilesB[t]
                xf, xbb, xTb, sg, yin, yT, mg, mgb, mT, hpre, scr, hout, stB = (xf2[k_], xbb2[k_], xTb2[k_], sg2[k_], yin2[k_], yT2[k_], mg2[k_],
                                                                               mgb2[k_], mT2[k_], hpre2[k_], scr2b[k_], hout2[k_], stB2[k_])
                rows = slice(i * 128, (i + 1) * 128)
                transposes(mgb, 8, mT)
                for g in range(2):
                    bo = em.bank()
                    for c in range(8):
                        em.op('pe', lambda e, c=c, g=g, bo=bo, mT=mT: e.matmul(bo.h[:, :], lhsT=mT[:, c, :], rhs=Wo[:, c, g * 512:(g + 1) * 512], start=(c == 0), stop=(c == 7)),
                              r=[mT, Wo], w=[bo], inc=(c == 7))
                    cs = slice(g * 512, (g + 1) * 512)
                    em.op('dve', lambda e, bo=bo, cs=cs, hpre=hpre, xf=xf: e.scalar_tensor_tensor(out=hpre[:, cs], in0=xf[:, cs], scalar=ALPHA, in1=bo.h[:, :], op0=ALU.mult, op1=ALU.add), r=[xf, bo], w=[hpre])
                layer_norm(hpre, hout, g1, b1_, stB, epsB[:, 0:1], scr)
                em.dma('pool', d_h[s, rows, :], hout[:], r=[hout], w=[db['h'][s][i]])

            b1_s1(0)
            for t in range(len(tilesB)):
                if t + 1 < len(tilesB):
                    b1_s1(t + 1)
                b1_s2(t)
            em.flush(pst)

    if 'C' in passes:
        with ExitStack() as pst:
            cur_stack[0] = pst
            identB[0] = sb("identC", [128, 128], BF16)
            em.dma('pool', identB[0][:], cin['ident'], w=[identB[0]])
            epsC = sb("epsC", [128, 1])
            em.op('dve', lambda e: e.memset(epsC[:], LN_EPS), w=[epsC])
            Wfg = sb("Wfg", [128, 8, DFF], BF16)
            Wfu = sb("Wfu", [128, 8, DFF], BF16)
            Wfd = sb("Wfd", [128, 22, D], BF16)
            for c in range(8):
                em.dma('pool', Wfg[:, c, :], w_ffn_gate.rearrange("(c p) n -> p c n", p=128)[:, c, :], w=[Wfg])
                em.dma('pool', Wfu[:, c, :], w_ffn_up.rearrange("(c p) n -> p c n", p=128)[:, c, :], w=[Wfu])
            for c in range(22):
                em.dma('pool', Wfd[:, c, :], w_ffn_down.rearrange("(c p) n -> p c n", p=128)[:, c, :], w=[Wfd])
            g2 = sb("g2", [128, D])
            b2_ = sb("b2_", [128, D])
            bcast_load(g2, ln2_g, D)
            bcast_load(b2_, ln2_b, D)
            def dbl(name, shape, dt=F32):
                return [sb("%s_%d" % (name, k), shape, dt) for k in range(2)]
            hf2 = dbl("hf", [128, D])
            hbb2 = dbl("hbb", [128, D], BF16)
            hT2 = dbl("hT", [128, 8, 128], BF16)
            sil2 = dbl("sil", [128, 512])
            actb2 = dbl("actb", [128, DFF], BF16)
            aT2 = dbl("aT", [128, 22, 128], BF16)
            opre = sb("opre", [128, D])
            scr2 = sb("scr2", [128, D])
            oout2 = dbl("oout", [128, D])
            stC2 = dbl("stC", [128, 4])
            tilesC = [(s, i) for s in range(NSEQ) for i in range(NT)]
            gcn = [0]

            def b2_s1(t):
                k_ = t % 2
                s, i = tilesC[t]
                hf, hbb, hT, actb, aT, oout, stC = hf2[k_], hbb2[k_], hT2[k_], actb2[k_], aT2[k_], oout2[k_], stC2[k_]
                rows = slice(i * 128, (i + 1) * 128)
                em.dma('sp', hf[:], d_h[s, rows, :], r=[db['h'][s][i]], w=[hf])
                em.op('act', lambda e, hbb=hbb, hf=hf: e.copy(out=hbb[:], in_=hf[:]), r=[hf], w=[hbb])
                transposes(hbb, 8, hT)
                for g in range(6):
                    n = 512 if g < 5 else 256
                    c0 = g * 512
                    bg = em.bank()
                    bu = em.bank()
                    sil = sil2[gcn[0] % 2]
                    gcn[0] += 1
                    for c in range(8):
                        em.op('pe', lambda e, c=c, bg=bg, c0=c0, n=n, hT=hT: e.matmul(bg.h[:, 0:n], lhsT=hT[:, c, :], rhs=Wfg[:, c, c0:c0 + n], start=(c == 0), stop=(c == 7)), r=[hT, Wfg], w=[bg], inc=(c == 7))
                    for c in range(8):
                        em.op('pe', lambda e, c=c, bu=bu, c0=c0, n=n, hT=hT: e.matmul(bu.h[:, 0:n], lhsT=hT[:, c, :], rhs=Wfu[:, c, c0:c0 + n], start=(c == 0), stop=(c == 7)), r=[hT, Wfu], w=[bu], inc=(c == 7))
                    em.op('act', lambda e, bg=bg, n=n, sil=sil: e.activation(out=sil[:, 0:n], in_=bg.h[:, 0:n], func=AF.Silu), r=[bg], w=[sil])
                    em.op('dve', lambda e, bu=bu, n=n, c0=c0, sil=sil, actb=actb: e.tensor_tensor(out=actb[:, c0:c0 + n], in0=bu.h[:, 0:n], in1=sil[:, 0:n], op=ALU.mult), r=[bu, sil], w=[actb])

            def b2_s2(t):
                k_ = t % 2
                s, i = tilesC[t]
                hf, hbb, hT, actb, aT, oout, stC = hf2[k_], hbb2[k_], hT2[k_], actb2[k_], aT2[k_], oout2[k_], stC2[k_]
                rows = slice(i * 128, (i + 1) * 128)
                transposes(actb, 22, aT)
                for g in range(2):
                    bo = em.bank()
                    for c in range(22):
                        em.op('pe', lambda e, c=c, g=g, bo=bo, aT=aT: e.matmul(bo.h[:, :], lhsT=aT[:, c, :], rhs=Wfd[:, c, g * 512:(g + 1) * 512], start=(c == 0), stop=(c == 21)),
                              r=[aT, Wfd], w=[bo], inc=(c == 21))
                    cs = slice(g * 512, (g + 1) * 512)
                    em.op('dve', lambda e, bo=bo, cs=cs, hf=hf: e.scalar_tensor_tensor(out=opre[:, cs], in0=hf[:, cs], scalar=ALPHA, in1=bo.h[:, :], op0=ALU.mult, op1=ALU.add), r=[hf, bo], w=[opre])
                layer_norm(opre, oout, g2, b2_, stC, epsC[:, 0:1], scr2)
                em.dma('pool', out[s, rows, :], oout[:], r=[oout], w=[Buf("out")])

            b2_s1(0)
            for t in range(len(tilesC)):
                if t + 1 < len(tilesC):
                    b2_s1(t + 1)
                b2_s2(t)
            em.flush(pst)

    if os.environ.get('CLR1'):
        em.clear_all()
    st.close()
    return nc


_NC_CACHE = {}


def kernel(**inputs):
    S, NSEQ, NCORE = 4096, 2, 8
    if 'nc' not in _NC_CACHE:
        _NC_CACHE['nc'] = build(S, NSEQ)
    nc = _NC_CACHE['nc']
    hc = host_consts()
    base = {}
    for k, v in inputs.items():
        a = np.ascontiguousarray(np.asarray(v))
        if k in ('x', 'positions'):
            continue
        base[k] = a.reshape(a.shape[1:]) if a.shape[0] == 1 else a
    base['r_k'] = base['r_k'].reshape(512)
    for k, v in hc.items():
        base['c_' + k] = v
    xs = np.ascontiguousarray(np.asarray(inputs['x'], dtype=np.float32))
    ps = np.ascontiguousarray(np.asarray(inputs['positions'], dtype=np.int32))
    in_maps = []
    for c in range(NCORE):
        m = dict(base)
        m['x'] = xs[c * NSEQ:(c + 1) * NSEQ]
        m['positions'] = ps[c * NSEQ:(c + 1) * NSEQ]
        in_maps.append(m)
    res = run_bass_kernel_spmd(nc, in_maps, core_ids=list(range(NCORE)))
    return np.concatenate([np.asarray(r['out']) for r in res.results], axis=0).astype(np.float32)
```
